# Optimizing a Trainium2 kernel written in Bass

```python
import jax, jax.numpy as jnp
from jax import lax
import numpy as np

D_MODEL = 2048
BATCH = 4
SEQ = 4096
DEPTH = 4

GRID_W = 64
CTX_LEN = 256
N_MOD = 9
D_FF = ((8 * D_MODEL // 3 + 127) // 128) * 128
MLA_HEADS = 4
MLA_Q_LORA = 448
MLA_KV_LORA = 128
MLA_NOPE_DIM = 128
MLA_ROPE_DIM = 64
MLA_QK_DIM = MLA_NOPE_DIM + MLA_ROPE_DIM
MLA_V_DIM = 128
GQA_HEADS = 4
GQA_KV_HEADS = 2
GQA_GROUP = GQA_HEADS // GQA_KV_HEADS
GQA_HEAD_DIM = 128
CONV_WIDTH = 512
CONV_K = 3
FOURIER_GROUPS = 4
FOURIER_GROUP_DIM = 128
FOURIER_WIDTH = FOURIER_GROUPS * FOURIER_GROUP_DIM
BRANCH_WIDTH = 512
N_BRANCHES = 4
IN_SPLITS = (MLA_Q_LORA, MLA_KV_LORA, MLA_ROPE_DIM,
             GQA_HEADS * GQA_HEAD_DIM, GQA_KV_HEADS * GQA_HEAD_DIM, GQA_KV_HEADS * GQA_HEAD_DIM,
             CONV_WIDTH, CONV_WIDTH, CONV_WIDTH, FOURIER_WIDTH)
D_IN = sum(IN_SPLITS)
Q_BLOCK = 128
ROPE_THETA = 10000.0
NORM_EPS = 1e-6

kernel_name = 'hybrid_mla_gqa_conv_fourier_dit_block'


def rms_norm(x, g):
    xf = x.astype(jnp.float32)
    y = xf * lax.rsqrt(jnp.mean(xf * xf, axis=-1, keepdims=True) + NORM_EPS)
    return (y * g.astype(jnp.float32)).astype(x.dtype)


def modulate(x, shift, scale):
    return x * (1 + scale) + shift


def ada_modulation(cond, w_ada, b_ada):
    m = jax.nn.silu(cond) @ w_ada + b_ada
    return m.reshape(cond.shape[0], 1, N_MOD, -1)


def swiglu(xn, wi, wo):
    g, u = jnp.split(xn @ wi, 2, axis=-1)
    return (jax.nn.silu(g) * u) @ wo


def axial_rope_tables(pos_row, pos_col, rot_dim):
    n = rot_dim // 4
    inv_freq = ROPE_THETA ** (-jnp.arange(n, dtype=jnp.float32) / n)
    ang = jnp.concatenate([pos_row[:, None] * inv_freq, pos_col[:, None] * inv_freq], axis=-1)
    return jnp.cos(ang), jnp.sin(ang)


def apply_rope(x, cos, sin):
    half = x.shape[-1] // 2
    x1, x2 = x[..., :half], x[..., half:]
    c = cos[:, None, :].astype(x.dtype)
    s = sin[:, None, :].astype(x.dtype)
    return jnp.concatenate([x1 * c - x2 * s, x2 * c + x1 * s], axis=-1)


def split_cols(z):
    idx = [int(i) for i in np.cumsum(IN_SPLITS)[:-1]]
    return jnp.split(z, idx, axis=-1)


def mla_project(zq, zkv, zpe, g_cq, w_uq, g_ckv, w_ukv, g_q, g_k, rope):
    B_, T = zq.shape[:2]
    q = (rms_norm(zq, g_cq) @ w_uq).reshape(B_, T, MLA_HEADS, MLA_QK_DIM)
    kv = (rms_norm(zkv, g_ckv) @ w_ukv).reshape(B_, T, MLA_HEADS, MLA_NOPE_DIM + MLA_V_DIM)
    k_nope, v = kv[..., :MLA_NOPE_DIM], kv[..., MLA_NOPE_DIM:]
    k_pe = jnp.broadcast_to(zpe[:, :, None, :], (B_, T, MLA_HEADS, MLA_ROPE_DIM))
    k = jnp.concatenate([k_nope, k_pe], axis=-1)
    q = rms_norm(q, g_q)
    k = rms_norm(k, g_k)
    if rope is not None:
        q = jnp.concatenate([q[..., :MLA_NOPE_DIM], apply_rope(q[..., MLA_NOPE_DIM:], *rope)], axis=-1)
        k = jnp.concatenate([k[..., :MLA_NOPE_DIM], apply_rope(k[..., MLA_NOPE_DIM:], *rope)], axis=-1)
    return q[:, :, :, None, :], k, v


def gqa_project(zq, zk, zv, g_q, g_k, rope):
    B_, T = zq.shape[:2]
    q = rms_norm(zq.reshape(B_, T, GQA_HEADS, GQA_HEAD_DIM), g_q)
    k = rms_norm(zk.reshape(B_, T, GQA_KV_HEADS, GQA_HEAD_DIM), g_k)
    v = zv.reshape(B_, T, GQA_KV_HEADS, GQA_HEAD_DIM)
    if rope is not None:
        q = apply_rope(q, *rope)
        k = apply_rope(k, *rope)
    return q.reshape(B_, T, GQA_KV_HEADS, GQA_GROUP, GQA_HEAD_DIM), k, v


def joint_attention(q, k, v, k_ctx, v_ctx, scale):
    B_, S, KH, G, Dk = q.shape
    n_blocks = S // Q_BLOCK
    qb = q.reshape(B_, n_blocks, Q_BLOCK, KH, G, Dk).swapaxes(0, 1)

    def attend_block(q_blk):
        s = jnp.concatenate([jnp.einsum('bqhgd,bkhd->bhgqk', q_blk, k),
                             jnp.einsum('bqhgd,bkhd->bhgqk', q_blk, k_ctx)], axis=-1)
        p = jax.nn.softmax(s.astype(jnp.float32) * scale, axis=-1).astype(v.dtype)
        return (jnp.einsum('bhgqk,bkhd->bqhgd', p[..., :S], v)
                + jnp.einsum('bhgqk,bkhd->bqhgd', p[..., S:], v_ctx))

    o = lax.map(attend_block, qb)
    return o.swapaxes(0, 1).reshape(B_, S, -1)


def context_attention(q, k, v, scale):
    B_, T = q.shape[:2]
    s = jnp.einsum('bqhgd,bkhd->bhgqk', q, k)
    p = jax.nn.softmax(s.astype(jnp.float32) * scale, axis=-1).astype(v.dtype)
    return jnp.einsum('bhgqk,bkhd->bqhgd', p, v).reshape(B_, T, -1)


def short_conv_mix(zb, zc, zx, w, b):
    u = zc * zx
    up = jnp.pad(u, ((0, 0), (1, 1), (0, 0)))
    y = up[:, :-2] * w[0] + up[:, 1:-1] * w[1] + up[:, 2:] * w[2] + b
    return zb * y


def fourier_mix(zf):
    B_, T = zf.shape[:2]
    f = zf.reshape(B_, T, FOURIER_GROUPS, FOURIER_GROUP_DIM).astype(jnp.float32)
    spec = jnp.fft.fft2(f, axes=(1, 3), norm='ortho').real
    return spec.reshape(B_, T, FOURIER_WIDTH).astype(zf.dtype)


def merge_branches(xn, branches, w_branch, w_gate, b_gate, w_o):
    merged = jax.nn.sigmoid(xn @ w_gate[0] + b_gate[0]) * (branches[0] @ w_branch[0])
    for i in range(1, N_BRANCHES):
        merged = merged + jax.nn.sigmoid(xn @ w_gate[i] + b_gate[i]) * (branches[i] @ w_branch[i])
    return merged @ w_o


def setup_inputs(seed: int = 0) -> dict:
    key = jax.random.key(seed)
    ks = jax.random.split(key, 32)
    L, D, F = DEPTH, D_MODEL, D_FF

    def nrm(k, shape, scale):
        return jax.random.normal(k, shape, jnp.float32) * scale

    def gain(k, shape):
        return 1.0 + 0.01 * jax.random.normal(k, shape, jnp.float32)

    return {
        'x': nrm(ks[0], (BATCH, SEQ, D), 1.0),
        'c': nrm(ks[1], (BATCH, D), 1.0),
        'ctx': nrm(ks[2], (BATCH, CTX_LEN, D), 1.0),
        'c_ctx': nrm(ks[3], (D,), 1.0),
        'w_ada': nrm(ks[4], (L, D, N_MOD * D), 0.5 * D ** -0.5),
        'b_ada': nrm(ks[5], (L, N_MOD * D), 0.01),
        'norm_ffn1': gain(ks[6], (L, D)),
        'ffn1_wi': nrm(ks[7], (L, D, 2 * F), D ** -0.5),
        'ffn1_wo': nrm(ks[8], (L, F, D), F ** -0.5),
        'norm_mix': gain(ks[9], (L, D)),
        'w_in': nrm(ks[10], (L, D, D_IN), D ** -0.5),
        'g_cq': gain(ks[11], (L, MLA_Q_LORA)),
        'w_uq': nrm(ks[12], (L, MLA_Q_LORA, MLA_HEADS * MLA_QK_DIM), MLA_Q_LORA ** -0.5),
        'g_ckv': gain(ks[13], (L, MLA_KV_LORA)),
        'w_ukv': nrm(ks[14], (L, MLA_KV_LORA, MLA_HEADS * (MLA_NOPE_DIM + MLA_V_DIM)), MLA_KV_LORA ** -0.5),
        'g_qa': gain(ks[15], (L, MLA_QK_DIM)),
        'g_ka': gain(ks[16], (L, MLA_QK_DIM)),
        'g_qb': gain(ks[17], (L, GQA_HEAD_DIM)),
        'g_kb': gain(ks[18], (L, GQA_HEAD_DIM)),
        'conv_w': nrm(ks[19], (L, CONV_K, CONV_WIDTH), CONV_K ** -0.5),
        'conv_b': nrm(ks[20], (L, CONV_WIDTH), 0.01),
        'w_branch': nrm(ks[21], (L, N_BRANCHES, BRANCH_WIDTH, D), BRANCH_WIDTH ** -0.5),
        'w_gate': nrm(ks[22], (L, N_BRANCHES, D, D), D ** -0.5),
        'b_gate': nrm(ks[23], (L, N_BRANCHES, D), 0.01),
        'w_o': nrm(ks[24], (L, D, D), D ** -0.5),
        'norm_ffn2': gain(ks[25], (L, D)),
        'ffn2_wi': nrm(ks[26], (L, D, 2 * F), D ** -0.5),
        'ffn2_wo': nrm(ks[27], (L, F, D), F ** -0.5),
    }


def reference(x, c, ctx, c_ctx, w_ada, b_ada, norm_ffn1, ffn1_wi, ffn1_wo, norm_mix, w_in,
              g_cq, w_uq, g_ckv, w_ukv, g_qa, g_ka, g_qb, g_kb, conv_w, conv_b,
              w_branch, w_gate, b_gate, w_o, norm_ffn2, ffn2_wi, ffn2_wo):
    S = x.shape[1]
    ROWS = S // GRID_W
    pos_row = jnp.broadcast_to(jnp.arange(ROWS, dtype=jnp.float32)[:, None], (ROWS, GRID_W)).reshape(-1)
    pos_col = jnp.broadcast_to(jnp.arange(GRID_W, dtype=jnp.float32)[None, :], (ROWS, GRID_W)).reshape(-1)
    rope_a = axial_rope_tables(pos_row, pos_col, MLA_ROPE_DIM)
    rope_b = axial_rope_tables(pos_row, pos_col, GQA_HEAD_DIM)
    scale_a = MLA_QK_DIM ** -0.5
    scale_b = GQA_HEAD_DIM ** -0.5

    h, hc = x, ctx
    for l in range(DEPTH):
        last = l == DEPTH - 1
        m = ada_modulation(c, w_ada[l], b_ada[l])
        mc = ada_modulation(c_ctx[None, :], w_ada[l], b_ada[l])

        h = h + 0.5 * m[:, :, 2] * swiglu(
            modulate(rms_norm(h, norm_ffn1[l]), m[:, :, 0], m[:, :, 1]), ffn1_wi[l], ffn1_wo[l])
        hc = hc + 0.5 * mc[:, :, 2] * swiglu(
            modulate(rms_norm(hc, norm_ffn1[l]), mc[:, :, 0], mc[:, :, 1]), ffn1_wi[l], ffn1_wo[l])

        xn = modulate(rms_norm(h, norm_mix[l]), m[:, :, 3], m[:, :, 4])
        xnc = modulate(rms_norm(hc, norm_mix[l]), mc[:, :, 3], mc[:, :, 4])
        aq, akv, ape, bq, bk, bv, cb, cg, cx, fz = split_cols(xn @ w_in[l])
        aqc, akvc, apec, bqc, bkc, bvc, cbc, cgc, cxc, fzc = split_cols(xnc @ w_in[l])

        qa, ka, va = mla_project(aq, akv, ape, g_cq[l], w_uq[l], g_ckv[l], w_ukv[l], g_qa[l], g_ka[l], rope_a)
        qac, kac, vac = mla_project(aqc, akvc, apec, g_cq[l], w_uq[l], g_ckv[l], w_ukv[l], g_qa[l], g_ka[l], None)
        qb, kb, vb = gqa_project(bq, bk, bv, g_qb[l], g_kb[l], rope_b)
        qbc, kbc, vbc = gqa_project(bqc, bkc, bvc, g_qb[l], g_kb[l], None)

        branches = [joint_attention(qa, ka, va, kac, vac, scale_a),
                    joint_attention(qb, kb, vb, kbc, vbc, scale_b),
                    short_conv_mix(cb, cg, cx, conv_w[l], conv_b[l]),
                    fourier_mix(fz)]
        h = h + m[:, :, 5] * merge_branches(xn, branches, w_branch[l], w_gate[l], b_gate[l], w_o[l])
        if not last:
            branches_c = [context_attention(qac, kac, vac, scale_a),
                          context_attention(qbc, kbc, vbc, scale_b),
                          short_conv_mix(cbc, cgc, cxc, conv_w[l], conv_b[l]),
                          fourier_mix(fzc)]
            hc = hc + mc[:, :, 5] * merge_branches(xnc, branches_c, w_branch[l], w_gate[l], b_gate[l], w_o[l])

        h = h + 0.5 * m[:, :, 8] * swiglu(
            modulate(rms_norm(h, norm_ffn2[l]), m[:, :, 6], m[:, :, 7]), ffn2_wi[l], ffn2_wo[l])
        if not last:
            hc = hc + 0.5 * mc[:, :, 8] * swiglu(
                modulate(rms_norm(hc, norm_ffn2[l]), mc[:, :, 6], mc[:, :, 7]), ffn2_wi[l], ffn2_wo[l])
    return h
```

```python
import numpy as np
import ml_dtypes
import concourse.bass as bass
import concourse.mybir as mybir
from concourse.bass_utils import run_bass_kernel_spmd

F32 = mybir.dt.float32
BF16 = mybir.dt.bfloat16
AF = mybir.ActivationFunctionType
ALU = mybir.AluOpType

D = 2048
DT = 16
FF = 5504
FT = 43
NLAT = 2048
NCTX = 256
NTOK = NLAT + NCTX
SEQ = 4096
EPS = 1e-6
GROUPS = [(0, 512, 0), (512, 512, 0), (1024, 512, 0), (1536, 512, 0), (2048, 256, 1)]

SECS = [("wi1", 88, 16), ("wo1", 16, 43), ("win", 32, 16), ("wg", 64, 16), ("wbr", 64, 4),
        ("wo", 16, 16), ("wi2", 88, 16), ("wo2", 16, 43), ("wuq", 8, 4), ("wukv", 8, 1)]
SEC_INFO = {}
_off = 0
for _n, _nc, _kt in SECS:
    SEC_INFO[_n] = (_off, _nc // 8, _kt * 128 * 128, _kt)
    _off += (_nc // 8) * _kt * 128 * 128
SHARD = _off
SHW = SHARD // 128

_c = 0
def _alloc(n):
    global _c
    o = _c
    _c += n
    return o
C_ONESD = _alloc(128); C_ONES448 = _alloc(128); C_ONES192 = _alloc(128); C_ONES128 = _alloc(128)
C_R64 = _alloc(128); C_R128 = _alloc(128)
C_CDFT = _alloc(256)
C_ONE1 = _alloc(128)
C_NORM1 = _alloc(64); C_NORMM = _alloc(64); C_NORM2 = _alloc(64)
C_GCQ = _alloc(16); C_GCKV = _alloc(4); C_GQAA = _alloc(4); C_GQAB = _alloc(4); C_GKAA = _alloc(4); C_GKAB = _alloc(4)
C_GQB = _alloc(4); C_GKB = _alloc(4)
C_CONVW = _alloc(48); C_CONVB = _alloc(16); C_BGATE = _alloc(256); C_BADA = _alloc(72)
C_OH = _alloc(4); C_HMASK = _alloc(2); C_EPS = _alloc(1)
NCONST = _c

X_KA = 0
X_KB = X_KA + 4 * 128 * NLAT
X_VA = X_KB + 4 * 64 * NLAT
X_KG = X_VA + NLAT * 512
X_VG = X_KG + 2 * 128 * NLAT
X_AB = X_VG + NLAT * 256
XSZ = X_AB + NLAT * 1024
assert XSZ % 128 == 0
XSZ2 = XSZ + 2048
UE = 131072
NU = (XSZ2 + UE - 1) // UE
assert XSZ % UE == 0


class Op:
    __slots__ = ("eng", "fn", "deps", "sig", "val", "is_dma", "key", "inc", "pending")


class Sched:
    ENGS = ["pe", "act", "dve", "pool", "sp"]

    def __init__(self):
        self.streams = {e: [] for e in self.ENGS}
        self.lw = {}
        self.rd = {}
        self.dma_last = {}
        self.dma_cnt = {}
        self.last_compute = {e: None for e in self.ENGS}
        self.pend = []

    def add(self, eng, fn, reads=(), writes=(), dma_key=None, inc=16, defer=False):
        op = Op()
        op.eng = eng; op.fn = fn; op.sig = False; op.val = 0
        op.is_dma = dma_key is not None; op.key = dma_key; op.inc = inc; op.pending = False
        deps = {}

        def need(d):
            if d is None or d is op:
                return
            if (not d.is_dma) and (not op.is_dma) and d.eng == eng and eng == "pe":
                return
            deps[id(d)] = d
        for k in reads:
            need(self.lw.get(k))
        for k in writes:
            need(self.lw.get(k))
            r = self.rd.get(k)
            if r:
                for x in r.values():
                    need(x)
        if op.is_dma:
            need(self.dma_last.get(dma_key))
            self.dma_cnt[dma_key] = self.dma_cnt.get(dma_key, 0) + inc
            op.val = self.dma_cnt[dma_key]
            self.dma_last[dma_key] = op
            op.sig = True
        else:
            self.last_compute[eng] = op
        op.deps = list(deps.values())
        for d in op.deps:
            d.sig = True
        rk = ("d", dma_key) if op.is_dma else eng
        for k in reads:
            self.rd.setdefault(k, {})[rk] = op
        for k in writes:
            self.lw[k] = op
            self.rd[k] = {}
        if any(d.pending for d in op.deps):
            self.flush()
        if defer:
            op.pending = True
            self.pend.append(op)
        else:
            self.streams[eng].append(op)
        return op

    def flush(self):
        for o in self.pend:
            o.pending = False
            self.streams[o.eng].append(o)
        self.pend = []

    def barrier(self):
        self.flush()
        lasts = [o for o in self.last_compute.values() if o is not None]
        dl = [o for o in self.dma_last.values() if o.key not in ("k_wag", "k_wcast")]
        for e in self.ENGS:
            op = Op()
            op.eng = e; op.fn = None; op.sig = False; op.val = 0; op.is_dma = False; op.key = None; op.inc = 0; op.pending = False
            op.deps = [o for o in lasts if not (o.eng == e and e == "pe")] + dl
            for d in op.deps:
                d.sig = True
            self.streams[e].append(op)
        self.lw = {k: v for k, v in self.lw.items() if isinstance(k, tuple) and k[0] in ("wfull", "wsb")}
        self.rd = {}

    def check(self):
        done = set()
        ptr = {e: 0 for e in self.ENGS}
        total = sum(len(v) for v in self.streams.values())
        n = 0
        progress = True
        while progress:
            progress = False
            for e in self.ENGS:
                st = self.streams[e]
                while ptr[e] < len(st):
                    op = st[ptr[e]]
                    if all(id(d) in done for d in op.deps):
                        done.add(id(op)); ptr[e] += 1; n += 1; progress = True
                    else:
                        break
        assert n == total, ("DEADLOCK in schedule", {e: (ptr[e], len(self.streams[e])) for e in self.ENGS})
        assert not self.pend

    def emit(self, nc, block, eng_objs, sem_of_eng, sem_of_key):
        for e in self.ENGS:
            cnt = 0
            for op in self.streams[e]:
                if op.fn is not None and (not op.is_dma) and op.sig:
                    cnt += 1
                    op.val = cnt

        def run_stream(e):
            def body(eng):
                waited = {}
                for op in self.streams[e]:
                    for d in op.deps:
                        if d.is_dma:
                            sem = sem_of_key[d.key]; sid = ("k", d.key)
                        else:
                            sem = sem_of_eng[d.eng]; sid = ("e", d.eng)
                        if waited.get(sid, 0) >= d.val:
                            continue
                        waited[sid] = d.val
                        eng.wait_ge(sem, d.val)
                    if op.fn is None:
                        continue
                    ins = op.fn(eng)
                    if op.is_dma:
                        if op.inc == 1:
                            ins.then_inc(sem_of_key[op.key])
                        else:
                            ins.then_inc(sem_of_key[op.key], op.inc)
                    elif op.sig:
                        ins.then_inc(sem_of_eng[e], 1)
            return body
        block.tensor(run_stream("pe"))
        block.scalar(run_stream("act"))
        block.vector(run_stream("dve"))
        block.gpsimd(run_stream("pool"))
        block.sync(run_stream("sp"))


class Rot:
    def __init__(self, aps, name):
        self.aps = aps; self.name = name; self.i = 0

    def next(self):
        j = self.i % len(self.aps)
        self.i += 1
        return self.aps[j], (self.name, j)


def build_program(depth=4, stop_after=None, debug=False):
    nc = bass.Bass("TRN2", target_bir_lowering=False)
    S = Sched()
    kind_dbg = "ExternalOutput" if debug else "Internal"

    def dram_in(name, shape, dt):
        return nc.dram_tensor(name, shape, dt, kind="ExternalInput").ap()

    def dram_tmp(name, shape, dt, dbg=False):
        if dbg and debug:
            return nc.dram_tensor(name, shape, dt, kind="ExternalOutput").ap()
        return nc.dram_tensor(name, shape, dt).ap()

    xT = dram_in("xT", [DT, 128, NLAT], F32)
    cxT = dram_in("cxT", [DT, 128, NCTX], F32)
    ccT = dram_in("ccT", [128, DT, 5], F32)
    consts = dram_in("consts", [128, NCONST], F32)
    wada = dram_in("wada", [depth * 18, 128, DT * 128], F32)
    wsh = dram_in("wsh", [depth, 128, SHW], F32)
    ropeA = dram_in("ropeA", [2, 128, NLAT], F32)
    ropeB = dram_in("ropeB", [2, 128, NLAT], F32)
    fcos = dram_in("fcos", [4, 128, 32, 512], BF16)
    fsin = dram_in("fsin", [4, 128, 32, 512], BF16)
    fcosc = dram_in("fcosc", [128, 2, 256], BF16)
    fsinc = dram_in("fsinc", [128, 2, 256], BF16)
    outT = nc.dram_tensor("outT", [DT, 128, NLAT], F32, kind="ExternalOutput").ap()

    hT = dram_tmp("hT", [DT, 128, NTOK], F32, dbg=True)
    xnT = dram_tmp("xnT", [DT, 128, NTOK], BF16, dbg=True)
    QA = dram_tmp("QA", [4, 128, NTOK], BF16, dbg=True)
    QB = dram_tmp("QB", [4, 64, NTOK], BF16, dbg=True)
    QG = dram_tmp("QG", [4, 128, NTOK], BF16, dbg=True)
    KAc = dram_tmp("KAc", [4, 128, NCTX], BF16)
    KBc = dram_tmp("KBc", [4, 64, NCTX], BF16)
    VAc = dram_tmp("VAc", [NCTX, 512], BF16)
    KGc = dram_tmp("KGc", [2, 128, NCTX], BF16)
    VGc = dram_tmp("VGc", [NCTX, 256], BF16)
    ABc = dram_tmp("ABc", [NCTX, 1024], BF16)
    CB = dram_tmp("CB", [4, 128, NTOK], F32)
    UU = dram_tmp("UU", [4, 128, NLAT + 2], F32)
    UUc = dram_tmp("UUc", [4, 128, NCTX + 2], F32)
    BR = dram_tmp("BR", [16, 128, NTOK], BF16, dbg=True)
    XI = [dram_tmp("XI%d" % i, [NU * 128, UE // 128], BF16) for i in range(2)]
    XO = [dram_tmp("XO%d" % i, [NU * 256, UE // 128], BF16) for i in range(2)]
    adaloc = dram_tmp("adaloc", [128, depth * 90], F32)
    adafull = dram_tmp("adafull", [8 * 128, depth * 90], F32)
    _wsb = [dram_tmp("wsb%d" % l, [128, SHW], BF16) for l in range(min(depth, 2))]
    _wfull = [dram_tmp("wfull%d" % l, [8 * 128, SHW], BF16) for l in range(min(depth, 2))]
    wsb = [_wsb[l % 2] for l in range(depth)]
    wfull = [_wfull[l % 2] for l in range(depth)]

    def xflat(t):
        return t.rearrange("a b -> (a b)")

    def wchunk(l, sec, c):
        off, per, celems, kt = SEC_INFO[sec]
        base = (c // per) * SHARD + off + (c % per) * celems
        return xflat(wfull[l])[base:base + celems].rearrange("(p x) -> p x", p=128)

    PAIRS = [[0, 1], [2, 3], [4, 5], [6, 7]]
    ALL8 = [list(range(8))]

    import contextlib
    with contextlib.ExitStack() as es:
        arena = es.enter_context(nc.sbuf_tensor("arena", [128, 49152], F32))
        psum = es.enter_context(nc.psum_tensor("psum", [128, 8, 512], F32))
        block = None

        class Arena:
            def __init__(self):
                self.off = 0

            def f32(self, n):
                o = self.off; self.off += n
                assert self.off <= 49152, self.off
                return arena[:, o:o + n]

            def bf(self, n):
                w = (n + 1) // 2
                o = self.off; self.off += w
                assert self.off <= 49152, self.off
                return arena[:, o:o + w].bitcast(BF16)[:, 0:n]
        A = Arena()
        cst = A.f32(NCONST)
        mvL = A.f32(depth * 144)
        mvC = A.f32(depth * 144)
        cdft_b = A.bf(256)
        ones_b = A.bf(128)
        hal = A.f32(8)
        rsq_tmp = A.f32(512)
        PERSIST = A.off

        def PS(b):
            return psum[:, b, :]

        def pk(b):
            return ("ps", b)

        def dma(q, out, in_, reads, writes, key):
            return S.add(q, lambda e: e.dma_start(out=out, in_=in_), reads, writes, dma_key=key)

        def store(out, in_, reads, writes, key):
            S.add("sp", lambda e: e.dma_start(out=out, in_=in_), reads, writes, dma_key=key, defer=True)

        def flush():
            S.flush()

        def mm(out, lhsT, rhs, start, stop, reads, writes):
            return S.add("pe", lambda e: e.matmul(out, lhsT, rhs, start=start, stop=stop), reads, writes)

        def act(out, in_, func, reads, writes, bias=None, scale=None):
            kw = {}
            if bias is not None:
                kw["bias"] = bias
            if scale is not None:
                kw["scale"] = scale
            return S.add("act", lambda e: e.activation(out=out, in_=in_, func=func, **kw), reads, writes)

        def tt(eng, out, in0, in1, op, reads, writes):
            return S.add(eng, lambda e: e.tensor_tensor(out=out, in0=in0, in1=in1, op=op), reads, writes)

        def ts(eng, out, in0, s1, op0, reads, writes, s2=None, op1=None):
            if op1 is None:
                return S.add(eng, lambda e: e.tensor_scalar(out=out, in0=in0, scalar1=s1, scalar2=None, op0=op0), reads, writes)
            return S.add(eng, lambda e: e.tensor_scalar(out=out, in0=in0, scalar1=s1, scalar2=s2, op0=op0, op1=op1), reads, writes)

        def stt(out, in0, scalar, in1, op0, op1, reads, writes):
            return S.add("dve", lambda e: e.scalar_tensor_tensor(out=out, in0=in0, scalar=scalar, in1=in1, op0=op0, op1=op1), reads, writes)

        def rstd_from(psb, N, out, okey):
            act(rsq_tmp[:, :N], PS(psb)[:, :N], AF.Sqrt, [pk(psb)], ["rsqtmp"], bias=cc(C_EPS))
            S.add("dve", lambda e, o=out[:, :N], i=rsq_tmp[:, :N]: e.reciprocal(out=o, in_=i), ["rsqtmp"], [okey])

        def cc(n):
            return cst[:, n:n + 1]

        dma("sp", cst, consts, [], ["cst"], "k_cst")
        sc_raw = A.f32(DT * 5)
        sc = A.f32(DT * 5)
        dma("sp", sc_raw, ccT.rearrange("p a b -> p (a b)"), [], ["scraw"], "k_sc")
        act(sc, sc_raw, AF.Silu, ["scraw"], ["sc"])
        act(cdft_b, cst[:, C_CDFT:C_CDFT + 256], AF.Copy, ["cst"], ["cdft"])
        act(ones_b, cst[:, C_ONE1:C_ONE1 + 128], AF.Copy, ["cst"], ["onesb"])
        dma("pool", hT[:, :, 0:NLAT], xT, [], ["hT"], "k_hinit")
        dma("pool", hT[:, :, NLAT:NTOK], cxT, [], ["hT"], "k_hinit")

        def weights_stage(l):
            half = SHW // 2
            dma("pool", wsb[l][:, 0:half], wsh[l, :, 0:half], [], [("wsb", l)], "k_wcast")
            dma("pool", wsb[l][:, half:SHW], wsh[l, :, half:SHW], [], [("wsb", l)], "k_wcast")
            S.add("pool", lambda e: e.collective_compute("AllGather", ALU.bypass, replica_groups=ALL8,
                                                         ins=[wsb[l].opt()], outs=[wfull[l].opt()]),
                  [("wsb", l)], [("wfull", l)], dma_key="k_wag", inc=1)
        weights_stage(0)

        stage = A.f32(depth * 90)
        wa_slots = [A.f32(DT * 128) for _ in range(4)]
        wa = Rot(wa_slots, "wa")
        pbank = 0
        for l in range(depth):
            for j in range(18):
                wap, wk = wa.next()
                dma("sp", wap, wada[l * 18 + j], [], [wk], "k_" + wk[0] + str(wk[1]))
                b = pbank % 8; pbank += 1
                for kt in range(DT):
                    mm(PS(b)[:, 0:5], wap[:, kt * 128:(kt + 1) * 128], sc[:, kt * 5:(kt + 1) * 5],
                       kt == 0, kt == DT - 1, [wk, "sc"], [pk(b)])
                o = l * 90 + j * 5
                ts("dve", stage[:, o:o + 5], PS(b)[:, 0:5], cc(C_BADA + l * 18 + j), ALU.add, [pk(b), "cst"], ["stage"])
        store(adaloc, stage, ["stage"], ["adaloc"], "k_adast")
        S.add("pool", lambda e: e.collective_compute("AllGather", ALU.bypass, replica_groups=ALL8,
                                                     ins=[adaloc.opt()], outs=[adafull.opt()]),
              ["adaloc"], ["adafull"], dma_key="k_adaag", inc=1)
        modT = A.f32(8 * depth * 90)
        dma("sp", modT.rearrange("p (r x) -> p r x", r=8), adafull.rearrange("(r p) x -> p r x", p=128),
            ["adafull"], ["modT"], "k_modT")
        modv = modT.rearrange("p (r x) -> p r x", r=8)
        for l in range(depth):
            def X(q):
                return modv[:, :, l * 90:(l + 1) * 90].rearrange("p r (j q) -> p r j q", q=5)[:, :, :, q]
            oL = mvL[:, l * 144:(l + 1) * 144].rearrange("p (r j) -> p r j", r=8)
            oC = mvC[:, l * 144:(l + 1) * 144].rearrange("p (r j) -> p r j", r=8)
            ts("dve", oL, X(0), cc(C_OH + 0), ALU.mult, ["modT", "cst"], ["mvL"])
            for q in range(1, 4):
                stt(oL, X(q), cc(C_OH + q), oL, ALU.mult, ALU.add, ["modT", "mvL"], ["mvL"])
            S.add("dve", lambda e, oC=oC, x4=X(4): e.tensor_copy(out=oC, in_=x4), ["modT"], ["mvC"])
            for mv in (mvL, mvC):
                nm = "mvL" if mv is mvL else "mvC"
                for m, ncol in ((1, C_NORM1), (4, C_NORMM), (7, C_NORM2)):
                    sl = mv[:, l * 144 + m * 16:l * 144 + (m + 1) * 16]
                    stt(sl, sl, 1.0, cst[:, ncol + l * 16:ncol + (l + 1) * 16], ALU.add, ALU.mult, [nm], [nm])
                for m in (2, 8):
                    sl = mv[:, l * 144 + m * 16:l * 144 + (m + 1) * 16]
                    ts("dve", sl, sl, 0.5, ALU.mult, [nm], [nm])
        S.barrier()

        def MV(isctx, l, m, dt):
            t = mvC if isctx else mvL
            o = l * 144 + m * 16 + dt
            return t[:, o:o + 1]

        def norm_mod(hs, xn, N, l, isctx, m_shift, m_scale, sq_rot, rstd, tmp_rot):
            for dt in range(DT):
                sq, sqk = sq_rot.next()
                act(sq[:, :N], hs[:, dt, :N], AF.Square, [("hs", dt)], [sqk])
                mm(PS(7)[:, :N], cst[:, C_ONESD:C_ONESD + 128], sq[:, :N], dt == 0, dt == DT - 1, [sqk], [pk(7)])
            rstd_from(7, N, rstd, "rstd")
            for dt in range(DT):
                tm, tk = tmp_rot.next()
                tt("dve", tm[:, :N], hs[:, dt, :N], rstd[:, :N], ALU.mult, [("hs", dt), "rstd"], [tk])
                act(xn[:, dt, :N], tm[:, :N], AF.Identity, [tk], [("xn", dt)],
                    bias=MV(isctx, l, m_shift, dt), scale=MV(isctx, l, m_scale, dt))

        def hs_keys():
            return [("hs", dt) for dt in range(DT)]

        def xn_keys():
            return [("xn", dt) for dt in range(DT)]

        def ffn_phase(l, which, groups, final_out):
            A.off = PERSIST
            hs = A.f32(DT * 512).rearrange("p (a n) -> p a n", a=DT)
            xn = A.bf(DT * 512).rearrange("p (a n) -> p a n", a=DT)
            h1 = A.bf(FT * 512).rearrange("p (a n) -> p a n", a=FT)
            sq_rot = Rot([A.f32(512) for _ in range(2)], "sq")
            tmp_rot = Rot([A.f32(512) for _ in range(2)], "tmp")
            rstd = A.f32(512)
            sg_rot = Rot([A.f32(512) for _ in range(3)], "sg")
            wi_rot = Rot([A.bf(2 * DT * 128) for _ in range(4)], "wi")
            wo_rot = Rot([A.bf(FT * 128) for _ in range(3)], "wo")
            seci = "wi1" if which == 1 else "wi2"
            seco = "wo1" if which == 1 else "wo2"
            m0 = 0 if which == 1 else 6
            pb = 0
            for (t0, N, isctx) in groups:
                dma("sp", hs[:, :, :N], hT[:, :, t0:t0 + N].rearrange("a p n -> p a n"), [("hT", t0)], hs_keys(), "k_hs")
                norm_mod(hs, xn, N, l, isctx, m0, m0 + 1, sq_rot, rstd, tmp_rot)
                for ft in range(FT):
                    w, wk = wi_rot.next()
                    dma("sp", w[:, 0:DT * 128], wchunk(l, seci, ft), [("wfull", l)], [wk], "k_%s%d" % wk)
                    dma("sp", w[:, DT * 128:2 * DT * 128], wchunk(l, seci, FT + ft), [("wfull", l)], [wk], "k_%s%d" % wk)
                    bg = pb % 6; bu = (pb + 1) % 6; pb += 2
                    for kt in range(DT):
                        mm(PS(bg)[:, :N], w[:, kt * 128:(kt + 1) * 128], xn[:, kt, :N], kt == 0, kt == DT - 1,
                           [wk, ("xn", kt)], [pk(bg)])
                    for kt in range(DT):
                        mm(PS(bu)[:, :N], w[:, (DT + kt) * 128:(DT + kt + 1) * 128], xn[:, kt, :N], kt == 0, kt == DT - 1,
                           [wk, ("xn", kt)], [pk(bu)])
                    sg, sgk = sg_rot.next()
                    act(sg[:, :N], PS(bg)[:, :N], AF.Silu, [pk(bg)], [sgk])
                    tt("dve", h1[:, ft, :N], sg[:, :N], PS(bu)[:, :N], ALU.mult, [sgk, pk(bu)], [("h1", ft)])
                for dt in range(DT):
                    w, wk = wo_rot.next()
                    dma("sp", w, wchunk(l, seco, dt), [("wfull", l)], [wk], "k_%s%d" % wk)
                    b = 6 + (dt % 2)
                    for ft in range(FT):
                        mm(PS(b)[:, :N], w[:, ft * 128:(ft + 1) * 128], h1[:, ft, :N], ft == 0, ft == FT - 1,
                           [wk, ("h1", ft)], [pk(b)])
                    stt(hs[:, dt, :N], PS(b)[:, :N], MV(isctx, l, m0 + 2, dt), hs[:, dt, :N], ALU.mult, ALU.add,
                        [pk(b), ("hs", dt)], [("hs", dt)])
                if final_out and not isctx:
                    store(outT[:, :, t0:t0 + N].rearrange("a p n -> p a n"), hs[:, :, :N], hs_keys(), [("outT", t0)], "k_hst")
                else:
                    store(hT[:, :, t0:t0 + N].rearrange("a p n -> p a n"), hs[:, :, :N], hs_keys(), [("hT", t0)], "k_hst")
            S.barrier()

        def mixa_phase(l, par):
            A.off = PERSIST
            xi = XI[par]
            xif = xflat(xi)
            hs = A.f32(DT * 512).rearrange("p (a n) -> p a n", a=DT)
            xn = A.bf(DT * 512).rearrange("p (a n) -> p a n", a=DT)
            sq_rot = Rot([A.f32(512) for _ in range(3)], "sq")
            tmp_rot = Rot([A.f32(512) for _ in range(3)], "tmp")
            rstd = A.f32(512)
            rs2 = Rot([A.f32(512) for _ in range(2)], "rs2")
            win_rot = Rot([A.bf(DT * 128) for _ in range(4)], "win")
            wuq = A.bf(8 * 4 * 128).rearrange("p (c x) -> p c x", c=8)
            wukv = A.bf(8 * 128).rearrange("p (c x) -> p c x", c=8)
            zq = A.f32(4 * 512).rearrange("p (a n) -> p a n", a=4)
            cqn = A.bf(4 * 512).rearrange("p (a n) -> p a n", a=4)
            cgs = A.f32(4 * 512).rearrange("p (a n) -> p a n", a=4)
            rA = A.f32(2 * 512).rearrange("p (a n) -> p a n", a=2)
            rB = A.f32(2 * 512).rearrange("p (a n) -> p a n", a=2)
            raw_rot = Rot([A.f32(512) for _ in range(3)], "raw")
            qn_rot = Rot([A.f32(512) for _ in range(3)], "qn")
            t1_rot = Rot([A.f32(512) for _ in range(2)], "t1")
            t2_rot = Rot([A.f32(512) for _ in range(2)], "t2")
            zkv = A.f32(512); zpe = A.f32(512); sqpe = A.f32(512); base = A.f32(512); ropeb = A.f32(512)
            ckvn = A.bf(512); fzb = A.bf(512)
            st_rot = Rot([A.bf(512) for _ in range(6)], "st")
            stf_rot = Rot([A.f32(512) for _ in range(3)], "stf")
            halst = A.f32(8)
            S.add("dve", lambda e: e.memset(halst, 0.0), [], ["halst"])
            for c in range(8):
                dma("sp", wuq[:, c, :], wchunk(l, "wuq", c), [("wfull", l)], ["wuq"], "k_wuq")
            for c in range(8):
                dma("sp", wukv[:, c, :], wchunk(l, "wukv", c), [("wfull", l)], ["wukv"], "k_wukv")
            GL = l * 4
            pbr = [0]

            def nb():
                b = pbr[0] % 6; pbr[0] += 1
                return b

            def inproj(c, N):
                w, wk = win_rot.next()
                dma("sp", w, wchunk(l, "win", c), [("wfull", l)], [wk], "k_%s%d" % wk)
                flush()
                b = nb()
                for kt in range(DT):
                    mm(PS(b)[:, :N], w[:, kt * 128:(kt + 1) * 128], xn[:, kt, :N], kt == 0, kt == DT - 1,
                       [wk, ("xn", kt)], [pk(b)])
                return b

            def rope(src, srck, Rcol, tab, N, out_bf, outk):
                b = nb()
                mm(PS(b)[:, :N], cst[:, Rcol:Rcol + 128], src[:, :N], True, True, [srck], [pk(b)])
                t1, t1k = t1_rot.next()
                t2, t2k = t2_rot.next()
                tt("dve", t1[:, :N], src[:, :N], tab[:, 0, :N], ALU.mult, [srck, "ropetab"], [t1k])
                tt("dve", t2[:, :N], PS(b)[:, :N], tab[:, 1, :N], ALU.mult, [pk(b), "ropetab"], [t2k])
                tt("dve", out_bf[:, :N], t1[:, :N], t2[:, :N], ALU.add, [t1k, t2k], [outk])

            def store_fm(dst, src_bf, N, srck, rows=128):
                store(dst, src_bf[0:rows, :N], [srck], [("dr", id(dst))], "k_%s%d" % srck)

            for (t0, N, isctx) in GROUPS:
                dma("sp", hs[:, :, :N], hT[:, :, t0:t0 + N].rearrange("a p n -> p a n"), [("hT", t0)], hs_keys(), "k_hs")
                if not isctx:
                    dma("sp", rA[:, :, :N], ropeA[:, :, t0:t0 + N].rearrange("a p n -> p a n"), [], ["ropetab"], "k_rA")
                    dma("sp", rB[:, :, :N], ropeB[:, :, t0:t0 + N].rearrange("a p n -> p a n"), [], ["ropetab"], "k_rA")
                norm_mod(hs, xn, N, l, isctx, 3, 4, sq_rot, rstd, tmp_rot)
                store(xnT[:, :, t0:t0 + N].rearrange("a p n -> p a n"), xn[:, :, :N], xn_keys(), [("xnT", t0)], "k_xnst")
                tl = t0 - NLAT

                for c in range(4):
                    b = inproj(c, N)
                    act(zq[:, c, :N], PS(b)[:, :N], AF.Copy, [pk(b)], [("zq", c)])
                    sq, sqk = sq_rot.next()
                    act(sq[:, :N], PS(b)[:, :N], AF.Square, [pk(b)], [sqk])
                    mm(PS(6)[:, :N], cst[:, C_ONES448:C_ONES448 + 128], sq[:, :N], c == 0, c == 3, [sqk], [pk(6)])
                rstd_from(6, N, rstd, "rstd")
                for c in range(4):
                    stt(cqn[:, c, :N], zq[:, c, :N], cc(C_GCQ + GL + c), rstd[:, :N], ALU.mult, ALU.mult,
                        [("zq", c), "rstd"], [("cqn", c)])
                for h in range(4):
                    raws = []
                    for part in range(2):
                        b = nb()
                        for kt in range(4):
                            mm(PS(b)[:, :N], wuq[:, 2 * h + part, kt * 128:(kt + 1) * 128], cqn[:, kt, :N], kt == 0, kt == 3,
                               ["wuq", ("cqn", kt)], [pk(b)])
                        raw, rk = raw_rot.next()
                        act(raw[:, :N], PS(b)[:, :N], AF.Copy, [pk(b)], [rk])
                        sq, sqk = sq_rot.next()
                        act(sq[:, :N], PS(b)[:, :N], AF.Square, [pk(b)], [sqk])
                        mm(PS(7)[:, :N], cst[:, C_ONES192:C_ONES192 + 128], sq[:, :N], part == 0, part == 1, [sqk], [pk(7)])
                        raws.append((raw, rk))
                    r2, r2k = rs2.next()
                    rstd_from(7, N, r2, r2k)
                    st, stk = st_rot.next()
                    stt(st[:, :N], raws[0][0][:, :N], cc(C_GQAA + l), r2[:, :N], ALU.mult, ALU.mult, [raws[0][1], r2k], [stk])
                    store_fm(QA[h, :, t0:t0 + N], st, N, stk)
                    qn, qnk = qn_rot.next()
                    stt(qn[:, :N], raws[1][0][:, :N], cc(C_GQAB + l), r2[:, :N], ALU.mult, ALU.mult, [raws[1][1], r2k], [qnk])
                    st, stk = st_rot.next()
                    if not isctx:
                        rope(qn, qnk, C_R64, rA, N, st, stk)
                    else:
                        S.add("dve", lambda e, o=st[:, :N], i=qn[:, :N]: e.tensor_copy(out=o, in_=i), [qnk], [stk])
                    store_fm(QB[h, :, t0:t0 + N], st, N, stk, rows=64)

                b = inproj(4, N)
                act(zkv[:, :N], PS(b)[:, :N], AF.Copy, [pk(b)], ["zkv"])
                sq, sqk = sq_rot.next()
                act(sq[:, :N], PS(b)[:, :N], AF.Square, [pk(b)], [sqk])
                mm(PS(6)[:, :N], cst[:, C_ONES128:C_ONES128 + 128], sq[:, :N], True, True, [sqk], [pk(6)])
                rstd_from(6, N, rstd, "rstd")
                stt(ckvn[:, :N], zkv[:, :N], cc(C_GCKV + l), rstd[:, :N], ALU.mult, ALU.mult, ["zkv", "rstd"], ["ckvn"])
                b = inproj(5, N)
                act(zpe[:, :N], PS(b)[:, :N], AF.Copy, [pk(b)], ["zpe"])
                act(sqpe[:, :N], PS(b)[:, :N], AF.Square, [pk(b)], ["sqpe"])
                ts("dve", base[:, :N], zpe[:, :N], cc(C_GKAB + l), ALU.mult, ["zpe"], ["base"])
                if not isctx:
                    rope_f32 = True
                    bb = nb()
                    mm(PS(bb)[:, :N], cst[:, C_R64:C_R64 + 128], base[:, :N], True, True, ["base"], [pk(bb)])
                    t1, t1k = t1_rot.next(); t2, t2k = t2_rot.next()
                    tt("dve", t1[:, :N], base[:, :N], rA[:, 0, :N], ALU.mult, ["base", "ropetab"], [t1k])
                    tt("dve", t2[:, :N], PS(bb)[:, :N], rA[:, 1, :N], ALU.mult, [pk(bb), "ropetab"], [t2k])
                    tt("dve", ropeb[:, :N], t1[:, :N], t2[:, :N], ALU.add, [t1k, t2k], ["ropeb"])
                    kb_src, kb_k = ropeb, "ropeb"
                else:
                    kb_src, kb_k = base, "base"
                for h in range(4):
                    b = nb()
                    mm(PS(b)[:, :N], wukv[:, 2 * h, :], ckvn[:, :N], True, True, ["wukv", "ckvn"], [pk(b)])
                    raw, rk = raw_rot.next()
                    act(raw[:, :N], PS(b)[:, :N], AF.Copy, [pk(b)], [rk])
                    sq, sqk = sq_rot.next()
                    act(sq[:, :N], PS(b)[:, :N], AF.Square, [pk(b)], [sqk])
                    mm(PS(7)[:, :N], cst[:, C_ONES192:C_ONES192 + 128], sq[:, :N], True, False, [sqk], [pk(7)])
                    mm(PS(7)[:, :N], cst[:, C_ONES192:C_ONES192 + 128], sqpe[:, :N], False, True, ["sqpe"], [pk(7)])
                    r2, r2k = rs2.next()
                    rstd_from(7, N, r2, r2k)
                    st, stk = st_rot.next()
                    stt(st[:, :N], raw[:, :N], cc(C_GKAA + l), r2[:, :N], ALU.mult, ALU.mult, [rk, r2k], [stk])
                    if not isctx:
                        dst = xif[X_KA + h * 128 * NLAT:X_KA + (h + 1) * 128 * NLAT].rearrange("(p n) -> p n", p=128)[:, t0:t0 + N]
                    else:
                        dst = KAc[h, :, tl:tl + N]
                    store_fm(dst, st, N, stk)
                    st, stk = st_rot.next()
                    tt("dve", st[:, :N], kb_src[:, :N], r2[:, :N], ALU.mult, [kb_k, r2k], [stk])
                    if not isctx:
                        dst = xif[X_KB + h * 64 * NLAT:X_KB + (h + 1) * 64 * NLAT].rearrange("(p n) -> p n", p=64)[:, t0:t0 + N]
                    else:
                        dst = KBc[h, :, tl:tl + N]
                    store_fm(dst, st, N, stk, rows=64)
                for tsb in range(N // 128):
                    b = nb()
                    for h in range(4):
                        mm(PS(b)[:, h * 128:(h + 1) * 128], ckvn[:, tsb * 128:(tsb + 1) * 128], wukv[:, 2 * h + 1, :], True, True,
                           ["wukv", "ckvn"], [pk(b)])
                    st, stk = st_rot.next()
                    act(st[:, :512], PS(b)[:, :512], AF.Copy, [pk(b)], [stk])
                    if not isctx:
                        dst = xif[X_VA:X_VA + NLAT * 512].rearrange("(t d) -> t d", d=512)[t0 + tsb * 128:t0 + (tsb + 1) * 128, :]
                    else:
                        dst = VAc[tl + tsb * 128:tl + (tsb + 1) * 128, :]
                    store(dst, st[:, :512], [stk], [("dr", "va", t0, tsb)], "k_%s%d" % stk)

                for hh in range(6):
                    b = inproj(6 + hh, N)
                    raw, rk = raw_rot.next()
                    act(raw[:, :N], PS(b)[:, :N], AF.Copy, [pk(b)], [rk])
                    sq, sqk = sq_rot.next()
                    act(sq[:, :N], PS(b)[:, :N], AF.Square, [pk(b)], [sqk])
                    mm(PS(6)[:, :N], cst[:, C_ONES128:C_ONES128 + 128], sq[:, :N], True, True, [sqk], [pk(6)])
                    r2, r2k = rs2.next()
                    rstd_from(6, N, r2, r2k)
                    qn, qnk = qn_rot.next()
                    gcol = (C_GQB if hh < 4 else C_GKB) + l
                    stt(qn[:, :N], raw[:, :N], cc(gcol), r2[:, :N], ALU.mult, ALU.mult, [rk, r2k], [qnk])
                    st, stk = st_rot.next()
                    if not isctx:
                        rope(qn, qnk, C_R128, rB, N, st, stk)
                    else:
                        S.add("dve", lambda e, o=st[:, :N], i=qn[:, :N]: e.tensor_copy(out=o, in_=i), [qnk], [stk])
                    if hh < 4:
                        dst = QG[hh, :, t0:t0 + N]
                    elif not isctx:
                        kh = hh - 4
                        dst = xif[X_KG + kh * 128 * NLAT:X_KG + (kh + 1) * 128 * NLAT].rearrange("(p n) -> p n", p=128)[:, t0:t0 + N]
                    else:
                        dst = KGc[hh - 4, :, tl:tl + N]
                    store_fm(dst, st, N, stk)
                for cv in range(2):
                    w, wk = win_rot.next()
                    dma("sp", w, wchunk(l, "win", 12 + cv), [("wfull", l)], [wk], "k_%s%d" % wk)
                    b = nb()
                    nts = N // 128
                    for tsb in range(nts):
                        for kt in range(DT):
                            mm(PS(b)[:, tsb * 128:(tsb + 1) * 128], xn[:, kt, tsb * 128:(tsb + 1) * 128], w[:, kt * 128:(kt + 1) * 128],
                               kt == 0, kt == DT - 1, [wk, ("xn", kt)], [pk(b)])
                    st, stk = st_rot.next()
                    act(st[:, :N], PS(b)[:, :N], AF.Copy, [pk(b)], [stk])
                    if not isctx:
                        dstv = xif[X_VG:X_VG + NLAT * 256].rearrange("(t d) -> t d", d=256)[t0:t0 + N, cv * 128:(cv + 1) * 128]
                    else:
                        dstv = VGc[tl:tl + N, cv * 128:(cv + 1) * 128]
                    store(dstv.rearrange("(a p) d -> p a d", p=128), st[:, :N].rearrange("p (a d) -> p a d", d=128),
                        [stk], [("dr", "vg", t0, cv)], "k_%s%d" % stk)

                for ccx in range(4):
                    b = inproj(14 + ccx, N)
                    sf, sfk = stf_rot.next()
                    act(sf[:, :N], PS(b)[:, :N], AF.Copy, [pk(b)], [sfk])
                    store(CB[ccx, :, t0:t0 + N], sf[:, :N], [sfk], [("dr", "cb", t0, ccx)], "k_%s%d" % sfk)
                for ccx in range(4):
                    b = inproj(18 + ccx, N)
                    act(cgs[:, ccx, :N], PS(b)[:, :N], AF.Copy, [pk(b)], [("cgs", ccx)])
                for ccx in range(4):
                    b = inproj(22 + ccx, N)
                    sf, sfk = stf_rot.next()
                    tt("dve", sf[:, :N], cgs[:, ccx, :N], PS(b)[:, :N], ALU.mult, [("cgs", ccx), pk(b)], [sfk])
                    if not isctx:
                        store(UU[ccx, :, 1 + t0:1 + t0 + N], sf[:, :N], [sfk], [("dr", "uu", t0, ccx)], "k_%s%d" % sfk)
                        if t0 == 0:
                            S.add("dve", lambda e, o=halst[:, 2 * ccx:2 * ccx + 1], i=sf[:, 0:1]: e.tensor_copy(out=o, in_=i), [sfk, "halst"], ["halst"])
                        if t0 + N == NLAT:
                            S.add("dve", lambda e, o=halst[:, 2 * ccx + 1:2 * ccx + 2], i=sf[:, N - 1:N]: e.tensor_copy(out=o, in_=i), [sfk, "halst"], ["halst"])
                    else:
                        store(UUc[ccx, :, 1 + tl:1 + tl + N], sf[:, :N], [sfk], [("dr", "uu", t0, ccx)], "k_%s%d" % sfk)

                for G in range(4):
                    b = inproj(26 + G, N)
                    act(fzb[:, :N], PS(b)[:, :N], AF.Copy, [pk(b)], ["fzb"])
                    for half in range(max(1, N // 256)):
                        b2 = nb()
                        for t2i in range(2):
                            tsb = half * 2 + t2i
                            mm(PS(b2)[:, t2i * 256:(t2i + 1) * 256], fzb[:, tsb * 128:(tsb + 1) * 128], cdft_b, True, True,
                               ["fzb"], [pk(b2)])
                        st, stk = st_rot.next()
                        act(st[:, :512], PS(b2)[:, :512], AF.Copy, [pk(b2)], [stk])
                        if not isctx:
                            dsta = xif[X_AB:X_AB + NLAT * 1024].rearrange("(t d) -> t d", d=1024)[t0 + half * 256:t0 + (half + 1) * 256, G * 256:(G + 1) * 256]
                        else:
                            dsta = ABc[tl + half * 256:tl + (half + 1) * 256, G * 256:(G + 1) * 256]
                        store(dsta.rearrange("(a p) d -> p a d", p=128), st[:, :512].rearrange("p (a d) -> p a d", d=256),
                            [stk], [("dr", "ab", t0, G, half)], "k_%s%d" % stk)
            store(xif[XSZ:XSZ2].bitcast(F32).rearrange("(p x) -> p x", p=128), halst, ["halst"], [("HI", par)], "k_hist")
            S.barrier()
            for u in range(NU):
                S.add("pool", lambda e, u=u: e.collective_compute("AllGather", ALU.bypass, replica_groups=PAIRS,
                                                                  ins=[XI[par][u * 128:(u + 1) * 128, :].opt()],
                                                                  outs=[XO[par][u * 256:(u + 1) * 256, :].opt()]),
                      [], [("XO", par)], dma_key="k_xag", inc=1)
            if l + 1 < depth:
                weights_stage(l + 1)
            S.barrier()

        def attn_phase(l, par, do_ctx):
            A.off = PERSIST
            xof = xflat(XO[par])
            KA = Rot([A.bf(4352) for _ in range(2)], "KAs")
            KB = Rot([A.bf(4352) for _ in range(2)], "KBs")
            V = A.bf(34 * 512).rearrange("p (c d) -> p c d", d=512)
            Qa = Rot([A.bf(512) for _ in range(2)], "Qa")
            Qb = Rot([A.bf(512) for _ in range(2)], "Qb")
            PT = Rot([A.bf(512) for _ in range(4)], "PT")
            OS = Rot([A.bf(512) for _ in range(2)], "OS")
            rinv = Rot([A.f32(512) for _ in range(2)], "rinv")
            sb_i = [0]; ob_i = [0]

            def xo_ap(r, off, n):
                u = off // UE; w = off % UE
                assert w + n <= UE
                b0 = u * 2 * UE + r * UE + w
                return xof[b0:b0 + n]

            def load_k(dst, sec, h, rows, dk, ctxsrc):
                per = rows * NLAT
                for r in range(2):
                    for hh in range(per // UE):
                        src = xo_ap(r, sec + h * per + hh * UE, UE).rearrange("(p n) -> p n", n=NLAT)
                        dma("sp", dst[hh * 64:(hh + 1) * 64, r * NLAT:(r + 1) * NLAT], src, [("XO", par)], [dk],
                            "k_%s%d_%d" % (dk[0], dk[1], (r * 2 + hh) % 2))
                dma("sp", dst[0:rows, 2 * NLAT:2 * NLAT + NCTX], ctxsrc[h], [], [dk], "k_%s%d_c" % dk)

            def load_v(sec, width, ctxsrc):
                Vv = V[:, :, 0:width]
                cu = UE // width // 128
                for r in range(2):
                    for j in range(NLAT * width // UE):
                        src = xo_ap(r, sec + j * UE, UE).rearrange("(c p d) -> p c d", p=128, d=width)
                        dma("sp", Vv[:, r * 16 + j * cu:r * 16 + (j + 1) * cu, :], src, [("XO", par)], ["V"], "k_V%d" % (j % 2))
                dma("sp", Vv[:, 32:34, :], ctxsrc.rearrange("(c p) d -> p c d", p=128), [], ["V"], "k_Vc")

            def heads(branch, nh, scale, mla):
                for h in range(nh):
                    if mla or h % 2 == 0:
                        ka, kak = KA.next()
                        if mla:
                            load_k(ka, X_KA, h, 128, kak, KAc)
                            kb, kbk = KB.next()
                            load_k(kb, X_KB, h, 64, kbk, KBc)
                        else:
                            load_k(ka, X_KG, h // 2, 128, kak, KGc)
                    vh = h if mla else h // 2
                    for (t0, N, isctx) in GROUPS:
                        if isctx and not do_ctx:
                            continue
                        qa, qak = Qa.next()
                        dma("sp", qa[:, :N], (QA if mla else QG)[h, :, t0:t0 + N], [], [qak], "k_%s%d" % qak)
                        flush()
                        if mla:
                            qb, qbk = Qb.next()
                            dma("sp", qb[0:64, :N], QB[h, :, t0:t0 + N], [], [qbk], "k_%s%d" % qbk)
                        chunks = list(range(32, 34)) if isctx else list(range(34))
                        ob = 4 + (ob_i[0] % 2); lb = 6 + (ob_i[0] % 2); ob_i[0] += 1
                        def emit_S(kc):
                            sb = sb_i[0] % 4; sb_i[0] += 1
                            mm(PS(sb)[:, :N], ka[:, kc * 128:(kc + 1) * 128], qa[:, :N], True, not mla, [kak, qak], [pk(sb)])
                            if mla:
                                mm(PS(sb)[:, :N], kb[0:64, kc * 128:(kc + 1) * 128], qb[0:64, :N], False, True, [kbk, qbk], [pk(sb)])
                            return sb
                        LA = 2
                        sbs = [emit_S(kc) for kc in chunks[:LA]]
                        for ci, kc in enumerate(chunks):
                            if ci + LA < len(chunks):
                                sbs.append(emit_S(chunks[ci + LA]))
                            sb = sbs[ci]
                            pt, ptk = PT.next()
                            act(pt[:, :N], PS(sb)[:, :N], AF.Exp, [pk(sb)], [ptk], scale=scale)
                            first = ci == 0; last = ci == len(chunks) - 1
                            mm(PS(ob)[:, :N], V[:, kc, vh * 128:(vh + 1) * 128], pt[:, :N], first, last, ["V", ptk], [pk(ob)])
                            mm(PS(lb)[:, :N], ones_b, pt[:, :N], first, last, [ptk], [pk(lb)])
                        ri, rik = rinv.next()
                        S.add("dve", lambda e, o=ri[:, :N], i=PS(lb)[:, :N]: e.reciprocal(out=o, in_=i), [pk(lb)], [rik])
                        os_, osk = OS.next()
                        tt("dve", os_[:, :N], PS(ob)[:, :N], ri[:, :N], ALU.mult, [pk(ob), rik], [osk])
                        store(BR[branch * 4 + h, :, t0:t0 + N], os_[:, :N], [osk], [("dr", "br", branch, h, t0)], "k_%s%d" % osk)

            load_v(X_VA, 512, VAc)
            heads(0, 4, 192 ** -0.5, True)
            load_v(X_VG, 256, VGc)
            heads(1, 4, 128 ** -0.5, False)
            S.barrier()

        def cf_phase(l, par, do_ctx):
            A.off = PERSIST
            xof = xflat(XO[par])
            AB = A.bf(34 * 1024).rearrange("p (c d) -> p c d", d=1024)
            tabc = Rot([A.bf(8 * 512) for _ in range(2)], "tabc")
            tabs = Rot([A.bf(8 * 512) for _ in range(2)], "tabs")
            cb = A.f32(4 * 512).rearrange("p (a n) -> p a n", a=4)
            U3 = A.f32(4 * 514).rearrange("p (a n) -> p a n", a=4)
            ta = Rot([A.f32(512) for _ in range(2)], "ta")
            tb = Rot([A.f32(512) for _ in range(2)], "tb")
            st_rot = Rot([A.bf(512) for _ in range(4)], "st")
            hl = A.f32(16)
            def xo_ap(r, off, n):
                u = off // UE; w = off % UE
                assert w + n <= UE
                b0 = u * 2 * UE + r * UE + w
                return xof[b0:b0 + n]
            for r in range(2):
                for j in range(16):
                    src = xo_ap(r, X_AB + j * UE, UE).rearrange("(c p d) -> p c d", p=128, d=1024)
                    dma("sp", AB[:, r * 16 + j:r * 16 + j + 1, :], src, [("XO", par)], ["AB"], "k_AB%d" % (j % 2))
            dma("sp", AB[:, 32:34, :], ABc.rearrange("(c p) d -> p c d", p=128), [], ["AB"], "k_ABc")
            for r in range(2):
                dma("sp", hl[:, r * 8:(r + 1) * 8], xo_ap(r, XSZ, 2048).bitcast(F32).rearrange("(p x) -> p x", p=128),
                    [("XO", par)], ["hl"], "k_hl")
            hv = hal.rearrange("p (c x) -> p c x", x=2)
            hlv = hl.rearrange("p (r c x) -> p r c x", r=2, x=2)
            ts("dve", hv[:, :, 0], hlv[:, 0, :, 1], cc(C_HMASK), ALU.mult, ["hl"], ["hal"])
            ts("dve", hv[:, :, 1], hlv[:, 1, :, 0], cc(C_HMASK + 1), ALU.mult, ["hl", "hal"], ["hal"])
            for (t0, N, isctx) in GROUPS:
                if isctx and not do_ctx:
                    continue
                tl = t0 - NLAT
                dma("sp", cb[:, :, :N], CB[:, :, t0:t0 + N].rearrange("a p n -> p a n"), [], ["cb"], "k_cb")
                if not isctx:
                    dma("sp", U3[:, :, :N + 2], UU[:, :, t0:t0 + N + 2].rearrange("a p n -> p a n"), [], ["U3"], "k_u3")
                    if t0 == 0:
                        S.add("dve", lambda e: e.tensor_copy(out=U3[:, :, 0], in_=hv[:, :, 0]), ["hal", "U3"], ["U3"])
                    if t0 + N == NLAT:
                        S.add("dve", lambda e, N=N: e.tensor_copy(out=U3[:, :, N + 1], in_=hv[:, :, 1]), ["hal", "U3"], ["U3"])
                else:
                    dma("sp", U3[:, :, :N + 2], UUc[:, :, 0:N + 2].rearrange("a p n -> p a n"), [], ["U3"], "k_u3")
                    S.add("dve", lambda e: e.memset(U3[:, :, 0], 0.0), ["U3"], ["U3"])
                    S.add("dve", lambda e, N=N: e.memset(U3[:, :, N + 1], 0.0), ["U3"], ["U3"])
                for ccx in range(4):
                    wc = C_CONVW + l * 12
                    a1, a1k = ta.next(); b1, b1k = tb.next()
                    ts("dve", a1[:, :N], U3[:, ccx, 0:N], cc(wc + 0 * 4 + ccx), ALU.mult, ["U3"], [a1k])
                    stt(b1[:, :N], U3[:, ccx, 1:N + 1], cc(wc + 1 * 4 + ccx), a1[:, :N], ALU.mult, ALU.add, ["U3", a1k], [b1k])
                    a2, a2k = ta.next()
                    stt(a2[:, :N], U3[:, ccx, 2:N + 2], cc(wc + 2 * 4 + ccx), b1[:, :N], ALU.mult, ALU.add, ["U3", b1k], [a2k])
                    st, stk = st_rot.next()
                    stt(st[:, :N], a2[:, :N], cc(C_CONVB + l * 4 + ccx), cb[:, ccx, :N], ALU.add, ALU.mult, [a2k, "cb"], [stk])
                    store(BR[8 + ccx, :, t0:t0 + N], st[:, :N], [stk], [("dr", "br2", ccx, t0)], "k_%s%d" % stk)
                if not isctx:
                    g = t0 // 512
                    for blk in range(4):
                        tc_, tck = tabc.next(); tsn, tsk = tabs.next()
                        dma("sp", tc_.rearrange("p (c k) -> p c k", c=8), fcos[g, :, blk * 8:(blk + 1) * 8, :], [], [tck], "k_%s%d" % tck)
                        dma("sp", tsn.rearrange("p (c k) -> p c k", c=8), fsin[g, :, blk * 8:(blk + 1) * 8, :], [], [tsk], "k_%s%d" % tsk)
                        for ti in range(8):
                            tcn = blk * 8 + ti
                            for G in range(4):
                                mm(PS(G)[:, :N], AB[:, tcn, G * 256:G * 256 + 128], tc_[:, ti * 512:(ti + 1) * 512], tcn == 0, False,
                                   ["AB", tck], [pk(G)])
                                mm(PS(G)[:, :N], AB[:, tcn, G * 256 + 128:G * 256 + 256], tsn[:, ti * 512:(ti + 1) * 512], False, tcn == 31,
                                   ["AB", tsk], [pk(G)])
                else:
                    tc_, tck = tabc.next(); tsn, tsk = tabs.next()
                    dma("sp", tc_[:, 0:512].rearrange("p (c k) -> p c k", c=2), fcosc, [], [tck], "k_%s%d" % tck)
                    dma("sp", tsn[:, 0:512].rearrange("p (c k) -> p c k", c=2), fsinc, [], [tsk], "k_%s%d" % tsk)
                    for ti in range(2):
                        for G in range(4):
                            mm(PS(G)[:, :N], AB[:, 32 + ti, G * 256:G * 256 + 128], tc_[:, ti * 256:(ti + 1) * 256], ti == 0, False,
                               ["AB", tck], [pk(G)])
                            mm(PS(G)[:, :N], AB[:, 32 + ti, G * 256 + 128:G * 256 + 256], tsn[:, ti * 256:(ti + 1) * 256], False, ti == 1,
                               ["AB", tsk], [pk(G)])
                for G in range(4):
                    st, stk = st_rot.next()
                    act(st[:, :N], PS(G)[:, :N], AF.Copy, [pk(G)], [stk])
                    store(BR[12 + G, :, t0:t0 + N], st[:, :N], [stk], [("dr", "br3", G, t0)], "k_%s%d" % stk)
            S.barrier()

        def merge_phase(l, do_ctx):
            A.off = PERSIST
            hs = A.f32(DT * 512).rearrange("p (a n) -> p a n", a=DT)
            xn = A.bf(DT * 512).rearrange("p (a n) -> p a n", a=DT)
            br = A.bf(16 * 512).rearrange("p (a n) -> p a n", a=16)
            mg = A.bf(DT * 512).rearrange("p (a n) -> p a n", a=DT)
            wg_rot = Rot([A.bf(DT * 128) for _ in range(4)], "wg")
            wb_rot = Rot([A.bf(4 * 128) for _ in range(4)], "wb")
            wo_rot = Rot([A.bf(DT * 128) for _ in range(2)], "wo")
            sg_rot = Rot([A.f32(512) for _ in range(3)], "sg")
            acc_rot = Rot([A.f32(512) for _ in range(3)], "acc")
            tmp_rot = Rot([A.f32(512) for _ in range(3)], "tmp")
            pb = [0]
            for (t0, N, isctx) in GROUPS:
                if isctx and not do_ctx:
                    continue
                dma("sp", hs[:, :, :N], hT[:, :, t0:t0 + N].rearrange("a p n -> p a n"), [("hT", t0)], hs_keys(), "k_hs")
                dma("sp", xn[:, :, :N], xnT[:, :, t0:t0 + N].rearrange("a p n -> p a n"), [], xn_keys(), "k_xn")
                dma("sp", br[:, :, :N], BR[:, :, t0:t0 + N].rearrange("a p n -> p a n"), [], ["br"], "k_br")
                for dt in range(DT):
                    prev = None
                    for i in range(4):
                        w, wk = wg_rot.next()
                        dma("sp", w, wchunk(l, "wg", i * 16 + dt), [("wfull", l)], [wk], "k_%s%d" % wk)
                        w2, w2k = wb_rot.next()
                        dma("sp", w2, wchunk(l, "wbr", i * 16 + dt), [("wfull", l)], [w2k], "k_%s%d" % w2k)
                        bg = pb[0] % 6; bb = (pb[0] + 1) % 6; pb[0] += 2
                        for kt in range(DT):
                            mm(PS(bg)[:, :N], w[:, kt * 128:(kt + 1) * 128], xn[:, kt, :N], kt == 0, kt == DT - 1,
                               [wk, ("xn", kt)], [pk(bg)])
                        for c in range(4):
                            mm(PS(bb)[:, :N], w2[:, c * 128:(c + 1) * 128], br[:, i * 4 + c, :N], c == 0, c == 3,
                               [w2k, "br"], [pk(bb)])
                        sg, sgk = sg_rot.next()
                        act(sg[:, :N], PS(bg)[:, :N], AF.Sigmoid, [pk(bg)], [sgk], bias=cc(C_BGATE + l * 64 + i * 16 + dt))
                        if i == 0:
                            ac, ack = acc_rot.next()
                            tt("dve", ac[:, :N], sg[:, :N], PS(bb)[:, :N], ALU.mult, [sgk, pk(bb)], [ack])
                            prev = (ac, ack)
                        else:
                            tm, tmk = tmp_rot.next()
                            tt("dve", tm[:, :N], sg[:, :N], PS(bb)[:, :N], ALU.mult, [sgk, pk(bb)], [tmk])
                            if i < 3:
                                ac, ack = acc_rot.next()
                                tt("dve", ac[:, :N], prev[0][:, :N], tm[:, :N], ALU.add, [prev[1], tmk], [ack])
                                prev = (ac, ack)
                            else:
                                tt("dve", mg[:, dt, :N], prev[0][:, :N], tm[:, :N], ALU.add, [prev[1], tmk], [("mg", dt)])
                for dt in range(DT):
                    w, wk = wo_rot.next()
                    dma("sp", w, wchunk(l, "wo", dt), [("wfull", l)], [wk], "k_%s%d" % wk)
                    b = 6 + dt % 2
                    for kt in range(DT):
                        mm(PS(b)[:, :N], w[:, kt * 128:(kt + 1) * 128], mg[:, kt, :N], kt == 0, kt == DT - 1,
                           [wk, ("mg", kt)], [pk(b)])
                    stt(hs[:, dt, :N], PS(b)[:, :N], MV(isctx, l, 5, dt), hs[:, dt, :N], ALU.mult, ALU.add,
                        [pk(b), ("hs", dt)], [("hs", dt)])
                store(hT[:, :, t0:t0 + N].rearrange("a p n -> p a n"), hs[:, :, :N], hs_keys(), [("hT", t0)], "k_hst")
            S.barrier()

        phases = []
        for l in range(depth):
            last = l == depth - 1
            grp_all = GROUPS
            grp_lat = GROUPS[:4]
            phases.append(("ffn1", l, lambda l=l: ffn_phase(l, 1, grp_all, False)))
            phases.append(("mixa", l, lambda l=l: mixa_phase(l, l % 2)))
            phases.append(("attn", l, lambda l=l, last=last: attn_phase(l, l % 2, not last)))
            phases.append(("cf", l, lambda l=l, last=last: cf_phase(l, l % 2, not last)))
            phases.append(("merge", l, lambda l=l, last=last: merge_phase(l, not last)))
            phases.append(("ffn2", l, lambda l=l, last=last: ffn_phase(l, 2, grp_lat if last else grp_all, last)))
        for name, l, fn in phases:
            if stop_after is not None and stop_after[0] == "prologue":
                break
            fn()
            if stop_after is not None and (name, l) == tuple(stop_after):
                break
        if stop_after is not None:
            A.off = PERSIST
            S.barrier()
            store(outT, hT[:, :, 0:NLAT], [], ["outT_final"], "k_hst")
        S.barrier()

        S.check()
        keys = sorted(S.dma_cnt.keys())
        sems_k = {}
        for k in keys:
            sems_k[k] = es.enter_context(nc.semaphore("s_" + k))
        sems_e = {e: es.enter_context(nc.semaphore("e_" + e)) for e in Sched.ENGS}
        block = es.enter_context(nc.Block())
        S.emit(nc, block, None, sems_e, sems_k)
    return nc


def _tile_w(W, n_chunks_pad):
    K, N = W.shape
    KT = (K + 127) // 128
    NT = (N + 127) // 128
    Wp = np.zeros((KT * 128, n_chunks_pad * 128), np.float32)
    Wp[:K, :N] = W
    return np.ascontiguousarray(Wp.reshape(KT, 128, n_chunks_pad, 128).transpose(2, 1, 0, 3))


def _layer_sections(inp, l):
    secs = {}
    secs["wi1"] = _tile_w(inp["ffn1_wi"][l], 88)
    secs["wo1"] = _tile_w(inp["ffn1_wo"][l], 16)
    secs["wi2"] = _tile_w(inp["ffn2_wi"][l], 88)
    secs["wo2"] = _tile_w(inp["ffn2_wo"][l], 16)
    w_in = inp["w_in"][l]
    cols = np.zeros((D, 32 * 128), np.float32)
    cols[:, 0:448] = w_in[:, 0:448]
    cols[:, 512:640] = w_in[:, 448:576]
    cols[:, 640:704] = w_in[:, 576:640]
    cols[:, 768:768 + 3072] = w_in[:, 640:3712]
    secs["win"] = _tile_w(cols, 32)
    wg = inp["w_gate"][l]
    secs["wg"] = np.concatenate([_tile_w(wg[i], 16) for i in range(4)], 0)
    wb = inp["w_branch"][l]
    secs["wbr"] = np.concatenate([_tile_w(wb[i], 16) for i in range(4)], 0)
    secs["wo"] = _tile_w(inp["w_o"][l], 16)
    wuq = inp["w_uq"][l]
    c2 = np.zeros((448, 8 * 128), np.float32)
    for h in range(4):
        c2[:, (2 * h) * 128:(2 * h) * 128 + 128] = wuq[:, h * 192:h * 192 + 128]
        c2[:, (2 * h + 1) * 128:(2 * h + 1) * 128 + 64] = wuq[:, h * 192 + 128:h * 192 + 192]
    secs["wuq"] = _tile_w(c2, 8)
    secs["wukv"] = _tile_w(inp["w_ukv"][l], 8)
    return secs


def _prep_inputs(inp, depth):
    f32 = np.float32
    per_core = [dict() for _ in range(8)]
    wsh = np.zeros((8, depth, SHARD), f32)
    for l in range(depth):
        secs = _layer_sections(inp, l)
        for name, ncp, kt in SECS:
            off, per, celems, _ = SEC_INFO[name]
            arr = secs[name].reshape(ncp, celems)
            for r in range(8):
                wsh[r, l, off:off + per * celems] = arr[r * per:(r + 1) * per].reshape(-1)
        del secs
    wada_all = inp["w_ada"][:depth]
    base = np.zeros((128, NCONST), f32)
    base[:, C_ONESD:C_ONESD + 128] = 1.0 / 2048
    base[:, C_ONES448:C_ONES448 + 128] = 1.0 / 448
    base[:, C_ONES192:C_ONES192 + 128] = 1.0 / 192
    base[:, C_ONES128:C_ONES128 + 128] = 1.0 / 128
    base[:, C_ONE1:C_ONE1 + 128] = 1.0
    base[:, C_EPS] = EPS

    def rmat(half):
        Rm = np.zeros((128, 128), f32)
        for r in range(half):
            Rm[r, r + half] = -1.0
            Rm[r + half, r] = 1.0
        return Rm.T.copy()
    base[:, C_R64:C_R64 + 128] = rmat(32)
    base[:, C_R128:C_R128 + 128] = rmat(64)
    cidx = np.arange(128)
    ang = 2 * np.pi * np.outer(cidx, cidx) / 128.0
    base[:, C_CDFT:C_CDFT + 128] = np.cos(ang)
    base[:, C_CDFT + 128:C_CDFT + 256] = np.sin(ang)

    def fm(v):
        return v[:depth].reshape(depth, 16, 128).transpose(2, 0, 1).reshape(128, depth * 16)
    base[:, C_NORM1:C_NORM1 + depth * 16] = fm(inp["norm_ffn1"])
    base[:, C_NORMM:C_NORMM + depth * 16] = fm(inp["norm_mix"])
    base[:, C_NORM2:C_NORM2 + depth * 16] = fm(inp["norm_ffn2"])
    gcq = np.zeros((depth, 512), f32); gcq[:, :448] = inp["g_cq"][:depth]
    base[:, C_GCQ:C_GCQ + depth * 4] = gcq.reshape(depth, 4, 128).transpose(2, 0, 1).reshape(128, depth * 4)
    base[:, C_GCKV:C_GCKV + depth] = inp["g_ckv"][:depth].T
    base[:, C_GQAA:C_GQAA + depth] = inp["g_qa"][:depth, :128].T
    base[:64, C_GQAB:C_GQAB + depth] = inp["g_qa"][:depth, 128:].T
    base[:, C_GKAA:C_GKAA + depth] = inp["g_ka"][:depth, :128].T
    base[:64, C_GKAB:C_GKAB + depth] = inp["g_ka"][:depth, 128:].T
    base[:, C_GQB:C_GQB + depth] = inp["g_qb"][:depth].T
    base[:, C_GKB:C_GKB + depth] = inp["g_kb"][:depth].T
    cw = inp["conv_w"][:depth]
    base[:, C_CONVW:C_CONVW + depth * 12] = cw.reshape(depth, 3, 4, 128).transpose(3, 0, 1, 2).reshape(128, depth * 12)
    base[:, C_CONVB:C_CONVB + depth * 4] = inp["conv_b"][:depth].reshape(depth, 4, 128).transpose(2, 0, 1).reshape(128, depth * 4)
    bg = inp["b_gate"][:depth]
    base[:, C_BGATE:C_BGATE + depth * 64] = bg.reshape(depth, 4, 16, 128).transpose(3, 0, 1, 2).reshape(128, depth * 64)
    cc5 = np.concatenate([inp["c"], inp["c_ctx"][None, :]], 0)
    ccT = np.ascontiguousarray(cc5.T.reshape(16, 128, 5).transpose(1, 0, 2))
    pos = np.arange(SEQ)
    prow = (pos // 64).astype(f32); pcol = (pos % 64).astype(f32)

    def rope_tab(rot_dim):
        n = rot_dim // 4
        inv = (10000.0 ** (-np.arange(n, dtype=f32) / n)).astype(f32)
        ang = np.concatenate([prow[:, None] * inv, pcol[:, None] * inv], -1).astype(f32)
        c = np.cos(ang).astype(f32); s = np.sin(ang).astype(f32)
        half = rot_dim // 2
        tab = np.zeros((2, 128, SEQ), f32)
        tab[0, :half] = c.T; tab[0, half:rot_dim] = c.T
        tab[1, :half] = s.T; tab[1, half:rot_dim] = s.T
        return tab
    tabA = rope_tab(64); tabB = rope_tab(128)
    bf = ml_dtypes.bfloat16
    sc_l = 1.0 / np.sqrt(SEQ * 128.0)
    tt_ = np.arange(SEQ, dtype=np.int64)
    sc_c = 1.0 / np.sqrt(NCTX * 128.0)
    tcx = np.arange(NCTX, dtype=np.int64)
    angc = 2 * np.pi * ((np.outer(tcx, tcx)) % NCTX) / NCTX
    fcosc = (np.cos(angc) * sc_c).astype(f32).reshape(2, 128, 256).transpose(1, 0, 2).astype(bf)
    fsinc = (-np.sin(angc) * sc_c).astype(f32).reshape(2, 128, 256).transpose(1, 0, 2).astype(bf)
    for core in range(8):
        b = core // 2; s = core % 2
        d = per_core[core]
        d["wsh"] = wsh[core].reshape(depth, 128, SHW)
        xs = inp["x"][b, s * NLAT:(s + 1) * NLAT, :]
        d["xT"] = np.ascontiguousarray(xs.T.reshape(16, 128, NLAT))
        d["cxT"] = np.ascontiguousarray(inp["ctx"][b].T.reshape(16, 128, NCTX))
        d["ccT"] = ccT
        cst = base.copy()
        wa = wada_all[:, :, core * 2304:(core + 1) * 2304]
        d["wada"] = np.ascontiguousarray(wa.reshape(depth, 16, 128, 18, 128).transpose(0, 3, 2, 1, 4)).reshape(depth * 18, 128, 16 * 128)
        ba = inp["b_ada"][:depth, core * 2304:(core + 1) * 2304]
        cst[:, C_BADA:C_BADA + depth * 18] = ba.reshape(depth, 18, 128).transpose(2, 0, 1).reshape(128, depth * 18)
        cst[:, C_OH + b] = 1.0
        cst[:, C_HMASK + 0] = 1.0 if s == 1 else 0.0
        cst[:, C_HMASK + 1] = 1.0 if s == 0 else 0.0
        d["consts"] = cst
        d["ropeA"] = np.ascontiguousarray(tabA[:, :, s * NLAT:(s + 1) * NLAT])
        d["ropeB"] = np.ascontiguousarray(tabB[:, :, s * NLAT:(s + 1) * NLAT])
        kk = np.arange(s * NLAT, (s + 1) * NLAT, dtype=np.int64)
        ang = 2 * np.pi * ((np.outer(tt_, kk)) % SEQ) / SEQ
        co = (np.cos(ang) * sc_l).astype(f32); si = (-np.sin(ang) * sc_l).astype(f32)
        d["fcos"] = np.ascontiguousarray(co.reshape(32, 128, 4, 512).transpose(2, 1, 0, 3)).astype(bf)
        d["fsin"] = np.ascontiguousarray(si.reshape(32, 128, 4, 512).transpose(2, 1, 0, 3)).astype(bf)
        d["fcosc"] = np.ascontiguousarray(fcosc)
        d["fsinc"] = np.ascontiguousarray(fsinc)
    return per_core


_CACHE = {}


def run(inputs, depth=4, stop_after=None, debug=False):
    inp = {k: np.asarray(v) for k, v in inputs.items()}
    key = (depth, tuple(stop_after) if stop_after else None, debug)
    if key not in _CACHE:
        _CACHE[key] = build_program(depth, stop_after, debug)
    nc = _CACHE[key]
    in_maps = _prep_inputs(inp, depth)
    res = run_bass_kernel_spmd(nc, in_maps, core_ids=list(range(8)))
    return res


def kernel(**inputs):
    res = run(inputs, 4)
    out = np.zeros((4, SEQ, D), np.float32)
    for core in range(8):
        b = core // 2; s = core % 2
        oT = np.asarray(res.results[core]["outT"]).reshape(D, NLAT)
        out[b, s * NLAT:(s + 1) * NLAT, :] = oT.T
    return out
```

```python
import numpy as np
import ml_dtypes
import concourse.bass as bass
import concourse.mybir as mybir
from concourse.bass_utils import run_bass_kernel_spmd

F32 = mybir.dt.float32
BF16 = mybir.dt.bfloat16
AF = mybir.ActivationFunctionType
ALU = mybir.AluOpType

D = 2048
DT = 16
FF = 5504
FT = 43
NLAT = 2048
NCTX = 256
NTOK = NLAT + NCTX
SEQ = 4096
EPS = 1e-6
GROUPS = [(0, 512, 0), (512, 512, 0), (1024, 512, 0), (1536, 512, 0), (2048, 256, 1)]

SECS = [("wi1", 88, 16), ("wo1", 16, 43), ("win", 32, 16), ("wg", 64, 16), ("wbr", 64, 4),
        ("wo", 16, 16), ("wi2", 88, 16), ("wo2", 16, 43), ("wuq", 8, 4), ("wukv", 8, 1)]
SEC_INFO = {}
_off = 0
for _n, _nc, _kt in SECS:
    SEC_INFO[_n] = (_off, _nc // 8, _kt * 128 * 128, _kt)
    _off += (_nc // 8) * _kt * 128 * 128
SHARD = _off
SHW = SHARD // 128

_c = 0
def _alloc(n):
    global _c
    o = _c
    _c += n
    return o
C_ONESD = _alloc(128); C_ONES448 = _alloc(128); C_ONES192 = _alloc(128); C_ONES128 = _alloc(128)
C_R64 = _alloc(128); C_R128 = _alloc(128)
C_CDFT = _alloc(256)
C_ONE1 = _alloc(128)
C_NORM1 = _alloc(64); C_NORMM = _alloc(64); C_NORM2 = _alloc(64)
C_GCQ = _alloc(16); C_GCKV = _alloc(4); C_GQAA = _alloc(4); C_GQAB = _alloc(4); C_GKAA = _alloc(4); C_GKAB = _alloc(4)
C_GQB = _alloc(4); C_GKB = _alloc(4)
C_CONVW = _alloc(48); C_CONVB = _alloc(16); C_BGATE = _alloc(256); C_BADA = _alloc(72)
C_OH = _alloc(4); C_HMASK = _alloc(2); C_EPS = _alloc(1)
NCONST = _c

X_KA = 0
X_KB = X_KA + 4 * 128 * NLAT
X_VA = X_KB + 4 * 64 * NLAT
X_KG = X_VA + NLAT * 512
X_VG = X_KG + 2 * 128 * NLAT
X_AB = X_VG + NLAT * 256
XSZ = X_AB + NLAT * 1024
assert XSZ % 128 == 0
XSZ2 = XSZ + 2048
UE = 131072
NU = (XSZ2 + UE - 1) // UE
assert XSZ % UE == 0


class Op:
    __slots__ = ("eng", "fn", "deps", "sig", "val", "is_dma", "key", "inc", "pending")


class Sched:
    ENGS = ["pe", "act", "dve", "pool", "sp"]

    def __init__(self):
        self.streams = {e: [] for e in self.ENGS}
        self.lw = {}
        self.rd = {}
        self.dma_last = {}
        self.dma_cnt = {}
        self.last_compute = {e: None for e in self.ENGS}
        self.pend = []

    def add(self, eng, fn, reads=(), writes=(), dma_key=None, inc=16, defer=False):
        op = Op()
        op.eng = eng; op.fn = fn; op.sig = False; op.val = 0
        op.is_dma = dma_key is not None; op.key = dma_key; op.inc = inc; op.pending = False
        deps = {}

        def need(d):
            if d is None or d is op:
                return
            if (not d.is_dma) and (not op.is_dma) and d.eng == eng and eng == "pe":
                return
            deps[id(d)] = d
        for k in reads:
            need(self.lw.get(k))
        for k in writes:
            need(self.lw.get(k))
            r = self.rd.get(k)
            if r:
                for x in r.values():
                    need(x)
        if op.is_dma:
            need(self.dma_last.get(dma_key))
            self.dma_cnt[dma_key] = self.dma_cnt.get(dma_key, 0) + inc
            op.val = self.dma_cnt[dma_key]
            self.dma_last[dma_key] = op
            op.sig = True
        else:
            self.last_compute[eng] = op
        op.deps = list(deps.values())
        for d in op.deps:
            d.sig = True
        rk = ("d", dma_key) if op.is_dma else eng
        for k in reads:
            self.rd.setdefault(k, {})[rk] = op
        for k in writes:
            self.lw[k] = op
            self.rd[k] = {}
        if any(d.pending for d in op.deps):
            self.flush()
        if defer:
            op.pending = True
            self.pend.append(op)
        else:
            self.streams[eng].append(op)
        return op

    def flush(self):
        for o in self.pend:
            o.pending = False
            self.streams[o.eng].append(o)
        self.pend = []

    def barrier(self):
        self.flush()
        lasts = [o for o in self.last_compute.values() if o is not None]
        dl = [o for o in self.dma_last.values() if o.key not in ("k_wag", "k_wcast")]
        for e in self.ENGS:
            op = Op()
            op.eng = e; op.fn = None; op.sig = False; op.val = 0; op.is_dma = False; op.key = None; op.inc = 0; op.pending = False
            op.deps = [o for o in lasts if not (o.eng == e and e == "pe")] + dl
            for d in op.deps:
                d.sig = True
            self.streams[e].append(op)
        self.lw = {k: v for k, v in self.lw.items() if isinstance(k, tuple) and k[0] in ("wfull", "wsb")}
        self.rd = {}

    def check(self):
        done = set()
        ptr = {e: 0 for e in self.ENGS}
        total = sum(len(v) for v in self.streams.values())
        n = 0
        progress = True
        while progress:
            progress = False
            for e in self.ENGS:
                st = self.streams[e]
                while ptr[e] < len(st):
                    op = st[ptr[e]]
                    if all(id(d) in done for d in op.deps):
                        done.add(id(op)); ptr[e] += 1; n += 1; progress = True
                    else:
                        break
        assert n == total, ("DEADLOCK in schedule", {e: (ptr[e], len(self.streams[e])) for e in self.ENGS})
        assert not self.pend

    def emit(self, nc, block, eng_objs, sem_of_eng, sem_of_key):
        for e in self.ENGS:
            cnt = 0
            for op in self.streams[e]:
                if op.fn is not None and (not op.is_dma) and op.sig:
                    cnt += 1
                    op.val = cnt

        def run_stream(e):
            def body(eng):
                waited = {}
                for op in self.streams[e]:
                    for d in op.deps:
                        if d.is_dma:
                            sem = sem_of_key[d.key]; sid = ("k", d.key)
                        else:
                            sem = sem_of_eng[d.eng]; sid = ("e", d.eng)
                        if waited.get(sid, 0) >= d.val:
                            continue
                        waited[sid] = d.val
                        eng.wait_ge(sem, d.val)
                    if op.fn is None:
                        continue
                    ins = op.fn(eng)
                    if op.is_dma:
                        if op.inc == 1:
                            ins.then_inc(sem_of_key[op.key])
                        else:
                            ins.then_inc(sem_of_key[op.key], op.inc)
                    elif op.sig:
                        ins.then_inc(sem_of_eng[e], 1)
            return body
        block.tensor(run_stream("pe"))
        block.scalar(run_stream("act"))
        block.vector(run_stream("dve"))
        block.gpsimd(run_stream("pool"))
        block.sync(run_stream("sp"))


class Rot:
    def __init__(self, aps, name):
        self.aps = aps; self.name = name; self.i = 0

    def next(self):
        j = self.i % len(self.aps)
        self.i += 1
        return self.aps[j], (self.name, j)


def build_program(depth=4, stop_after=None, debug=False):
    nc = bass.Bass("TRN2", target_bir_lowering=False)
    S = Sched()
    kind_dbg = "ExternalOutput" if debug else "Internal"

    def dram_in(name, shape, dt):
        return nc.dram_tensor(name, shape, dt, kind="ExternalInput").ap()

    def dram_tmp(name, shape, dt, dbg=False):
        if dbg and debug:
            return nc.dram_tensor(name, shape, dt, kind="ExternalOutput").ap()
        return nc.dram_tensor(name, shape, dt).ap()

    xT = dram_in("xT", [DT, 128, NLAT], F32)
    cxT = dram_in("cxT", [DT, 128, NCTX], F32)
    ccT = dram_in("ccT", [128, DT, 5], F32)
    consts = dram_in("consts", [128, NCONST], F32)
    wada = dram_in("wada", [depth * 18, 128, DT * 128], F32)
    wsh = dram_in("wsh", [depth, 128, SHW], F32)
    ropeA = dram_in("ropeA", [2, 128, NLAT], F32)
    ropeB = dram_in("ropeB", [2, 128, NLAT], F32)
    fcos = dram_in("fcos", [4, 128, 32, 512], BF16)
    fsin = dram_in("fsin", [4, 128, 32, 512], BF16)
    fcosc = dram_in("fcosc", [128, 2, 256], BF16)
    fsinc = dram_in("fsinc", [128, 2, 256], BF16)
    outT = nc.dram_tensor("outT", [DT, 128, NLAT], F32, kind="ExternalOutput").ap()

    hT = dram_tmp("hT", [DT, 128, NTOK], F32, dbg=True)
    xnT = dram_tmp("xnT", [DT, 128, NTOK], BF16, dbg=True)
    QA = dram_tmp("QA", [4, 128, NTOK], BF16, dbg=True)
    QB = dram_tmp("QB", [4, 64, NTOK], BF16, dbg=True)
    QG = dram_tmp("QG", [4, 128, NTOK], BF16, dbg=True)
    KAc = dram_tmp("KAc", [4, 128, NCTX], BF16)
    KBc = dram_tmp("KBc", [4, 64, NCTX], BF16)
    VAc = dram_tmp("VAc", [NCTX, 512], BF16)
    KGc = dram_tmp("KGc", [2, 128, NCTX], BF16)
    VGc = dram_tmp("VGc", [NCTX, 256], BF16)
    ABc = dram_tmp("ABc", [NCTX, 1024], BF16)
    CB = dram_tmp("CB", [4, 128, NTOK], F32)
    UU = dram_tmp("UU", [4, 128, NLAT + 2], F32)
    UUc = dram_tmp("UUc", [4, 128, NCTX + 2], F32)
    BR = dram_tmp("BR", [16, 128, NTOK], BF16, dbg=True)
    XI = [dram_tmp("XI%d" % i, [NU * 128, UE // 128], BF16) for i in range(2)]
    XO = [dram_tmp("XO%d" % i, [NU * 256, UE // 128], BF16) for i in range(2)]
    adaloc = dram_tmp("adaloc", [128, depth * 90], F32)
    adafull = dram_tmp("adafull", [8 * 128, depth * 90], F32)
    _wsb = [dram_tmp("wsb%d" % l, [128, SHW], BF16) for l in range(min(depth, 2))]
    _wfull = [dram_tmp("wfull%d" % l, [8 * 128, SHW], BF16) for l in range(min(depth, 2))]
    wsb = [_wsb[l % 2] for l in range(depth)]
    wfull = [_wfull[l % 2] for l in range(depth)]

    def xflat(t):
        return t.rearrange("a b -> (a b)")

    def wchunk(l, sec, c):
        off, per, celems, kt = SEC_INFO[sec]
        base = (c // per) * SHARD + off + (c % per) * celems
        return xflat(wfull[l])[base:base + celems].rearrange("(p x) -> p x", p=128)

    PAIRS = [[0, 1], [2, 3], [4, 5], [6, 7]]
    ALL8 = [list(range(8))]

    import contextlib
    with contextlib.ExitStack() as es:
        arena = es.enter_context(nc.sbuf_tensor("arena", [128, 49152], F32))
        psum = es.enter_context(nc.psum_tensor("psum", [128, 8, 512], F32))
        block = None

        class Arena:
            def __init__(self):
                self.off = 0

            def f32(self, n):
                o = self.off; self.off += n
                assert self.off <= 49152, self.off
                return arena[:, o:o + n]

            def bf(self, n):
                w = (n + 1) // 2
                o = self.off; self.off += w
                assert self.off <= 49152, self.off
                return arena[:, o:o + w].bitcast(BF16)[:, 0:n]
        A = Arena()
        cst = A.f32(NCONST)
        mvL = A.f32(depth * 144)
        mvC = A.f32(depth * 144)
        cdft_b = A.bf(256)
        ones_b = A.bf(128)
        hal = A.f32(8)
        rsq_tmp = A.f32(512)
        PERSIST = A.off

        def PS(b):
            return psum[:, b, :]

        def pk(b):
            return ("ps", b)

        def dma(q, out, in_, reads, writes, key):
            return S.add(q, lambda e: e.dma_start(out=out, in_=in_), reads, writes, dma_key=key)

        def store(out, in_, reads, writes, key):
            S.add("sp", lambda e: e.dma_start(out=out, in_=in_), reads, writes, dma_key=key, defer=True)

        def flush():
            S.flush()

        def mm(out, lhsT, rhs, start, stop, reads, writes):
            return S.add("pe", lambda e: e.matmul(out, lhsT, rhs, start=start, stop=stop), reads, writes)

        def act(out, in_, func, reads, writes, bias=None, scale=None):
            kw = {}
            if bias is not None:
                kw["bias"] = bias
            if scale is not None:
                kw["scale"] = scale
            return S.add("act", lambda e: e.activation(out=out, in_=in_, func=func, **kw), reads, writes)

        def tt(eng, out, in0, in1, op, reads, writes):
            return S.add(eng, lambda e: e.tensor_tensor(out=out, in0=in0, in1=in1, op=op), reads, writes)

        def ts(eng, out, in0, s1, op0, reads, writes, s2=None, op1=None):
            if op1 is None:
                return S.add(eng, lambda e: e.tensor_scalar(out=out, in0=in0, scalar1=s1, scalar2=None, op0=op0), reads, writes)
            return S.add(eng, lambda e: e.tensor_scalar(out=out, in0=in0, scalar1=s1, scalar2=s2, op0=op0, op1=op1), reads, writes)

        def stt(out, in0, scalar, in1, op0, op1, reads, writes):
            return S.add("dve", lambda e: e.scalar_tensor_tensor(out=out, in0=in0, scalar=scalar, in1=in1, op0=op0, op1=op1), reads, writes)

        def rstd_from(psb, N, out, okey):
            act(rsq_tmp[:, :N], PS(psb)[:, :N], AF.Sqrt, [pk(psb)], ["rsqtmp"], bias=cc(C_EPS))
            S.add("dve", lambda e, o=out[:, :N], i=rsq_tmp[:, :N]: e.reciprocal(out=o, in_=i), ["rsqtmp"], [okey])

        def cc(n):
            return cst[:, n:n + 1]

        dma("sp", cst, consts, [], ["cst"], "k_cst")
        sc_raw = A.f32(DT * 5)
        sc = A.f32(DT * 5)
        dma("sp", sc_raw, ccT.rearrange("p a b -> p (a b)"), [], ["scraw"], "k_sc")
        act(sc, sc_raw, AF.Silu, ["scraw"], ["sc"])
        act(cdft_b, cst[:, C_CDFT:C_CDFT + 256], AF.Copy, ["cst"], ["cdft"])
        act(ones_b, cst[:, C_ONE1:C_ONE1 + 128], AF.Copy, ["cst"], ["onesb"])
        dma("pool", hT[:, :, 0:NLAT], xT, [], ["hT"], "k_hinit")
        dma("pool", hT[:, :, NLAT:NTOK], cxT, [], ["hT"], "k_hinit")

        def weights_stage(l):
            half = SHW // 2
            dma("pool", wsb[l][:, 0:half], wsh[l, :, 0:half], [], [("wsb", l)], "k_wcast")
            dma("pool", wsb[l][:, half:SHW], wsh[l, :, half:SHW], [], [("wsb", l)], "k_wcast")
            S.add("pool", lambda e: e.collective_compute("AllGather", ALU.bypass, replica_groups=ALL8,
                                                         ins=[wsb[l].opt()], outs=[wfull[l].opt()]),
                  [("wsb", l)], [("wfull", l)], dma_key="k_wag", inc=1)
        weights_stage(0)

        stage = A.f32(depth * 90)
        wa_slots = [A.f32(DT * 128) for _ in range(4)]
        wa = Rot(wa_slots, "wa")
        pbank = 0
        for l in range(depth):
            for j in range(18):
                wap, wk = wa.next()
                dma("sp", wap, wada[l * 18 + j], [], [wk], "k_" + wk[0] + str(wk[1]))
                b = pbank % 8; pbank += 1
                for kt in range(DT):
                    mm(PS(b)[:, 0:5], wap[:, kt * 128:(kt + 1) * 128], sc[:, kt * 5:(kt + 1) * 5],
                       kt == 0, kt == DT - 1, [wk, "sc"], [pk(b)])
                o = l * 90 + j * 5
                ts("dve", stage[:, o:o + 5], PS(b)[:, 0:5], cc(C_BADA + l * 18 + j), ALU.add, [pk(b), "cst"], ["stage"])
        store(adaloc, stage, ["stage"], ["adaloc"], "k_adast")
        S.add("pool", lambda e: e.collective_compute("AllGather", ALU.bypass, replica_groups=ALL8,
                                                     ins=[adaloc.opt()], outs=[adafull.opt()]),
              ["adaloc"], ["adafull"], dma_key="k_adaag", inc=1)
        modT = A.f32(8 * depth * 90)
        dma("sp", modT.rearrange("p (r x) -> p r x", r=8), adafull.rearrange("(r p) x -> p r x", p=128),
            ["adafull"], ["modT"], "k_modT")
        modv = modT.rearrange("p (r x) -> p r x", r=8)
        for l in range(depth):
            def X(q):
                return modv[:, :, l * 90:(l + 1) * 90].rearrange("p r (j q) -> p r j q", q=5)[:, :, :, q]
            oL = mvL[:, l * 144:(l + 1) * 144].rearrange("p (r j) -> p r j", r=8)
            oC = mvC[:, l * 144:(l + 1) * 144].rearrange("p (r j) -> p r j", r=8)
            ts("dve", oL, X(0), cc(C_OH + 0), ALU.mult, ["modT", "cst"], ["mvL"])
            for q in range(1, 4):
                stt(oL, X(q), cc(C_OH + q), oL, ALU.mult, ALU.add, ["modT", "mvL"], ["mvL"])
            S.add("dve", lambda e, oC=oC, x4=X(4): e.tensor_copy(out=oC, in_=x4), ["modT"], ["mvC"])
            for mv in (mvL, mvC):
                nm = "mvL" if mv is mvL else "mvC"
                for m, ncol in ((1, C_NORM1), (4, C_NORMM), (7, C_NORM2)):
                    sl = mv[:, l * 144 + m * 16:l * 144 + (m + 1) * 16]
                    stt(sl, sl, 1.0, cst[:, ncol + l * 16:ncol + (l + 1) * 16], ALU.add, ALU.mult, [nm], [nm])
                for m in (2, 8):
                    sl = mv[:, l * 144 + m * 16:l * 144 + (m + 1) * 16]
                    ts("dve", sl, sl, 0.5, ALU.mult, [nm], [nm])
        S.barrier()

        def MV(isctx, l, m, dt):
            t = mvC if isctx else mvL
            o = l * 144 + m * 16 + dt
            return t[:, o:o + 1]

        def norm_mod(hs, xn, N, l, isctx, m_shift, m_scale, sq_rot, rstd, tmp_rot):
            for dt in range(DT):
                sq, sqk = sq_rot.next()
                act(sq[:, :N], hs[:, dt, :N], AF.Square, [("hs", dt)], [sqk])
                mm(PS(7)[:, :N], cst[:, C_ONESD:C_ONESD + 128], sq[:, :N], dt == 0, dt == DT - 1, [sqk], [pk(7)])
            rstd_from(7, N, rstd, "rstd")
            for dt in range(DT):
                tm, tk = tmp_rot.next()
                tt("dve", tm[:, :N], hs[:, dt, :N], rstd[:, :N], ALU.mult, [("hs", dt), "rstd"], [tk])
                act(xn[:, dt, :N], tm[:, :N], AF.Identity, [tk], [("xn", dt)],
                    bias=MV(isctx, l, m_shift, dt), scale=MV(isctx, l, m_scale, dt))

        def hs_keys():
            return [("hs", dt) for dt in range(DT)]

        def xn_keys():
            return [("xn", dt) for dt in range(DT)]

        def ffn_phase(l, which, groups, final_out):
            A.off = PERSIST
            hs = A.f32(DT * 512).rearrange("p (a n) -> p a n", a=DT)
            xn = A.bf(DT * 512).rearrange("p (a n) -> p a n", a=DT)
            h1 = A.bf(FT * 512).rearrange("p (a n) -> p a n", a=FT)
            sq_rot = Rot([A.f32(512) for _ in range(2)], "sq")
            tmp_rot = Rot([A.f32(512) for _ in range(2)], "tmp")
            rstd = A.f32(512)
            sg_rot = Rot([A.f32(512) for _ in range(3)], "sg")
            wi_rot = Rot([A.bf(2 * DT * 128) for _ in range(4)], "wi")
            wo_rot = Rot([A.bf(FT * 128) for _ in range(3)], "wo")
            seci = "wi1" if which == 1 else "wi2"
            seco = "wo1" if which == 1 else "wo2"
            m0 = 0 if which == 1 else 6
            pb = 0
            for (t0, N, isctx) in groups:
                dma("sp", hs[:, :, :N], hT[:, :, t0:t0 + N].rearrange("a p n -> p a n"), [("hT", t0)], hs_keys(), "k_hs")
                norm_mod(hs, xn, N, l, isctx, m0, m0 + 1, sq_rot, rstd, tmp_rot)
                for ft in range(FT):
                    w, wk = wi_rot.next()
                    wkg = (wk[0] + "g", wk[1]); wku = (wk[0] + "u", wk[1])
                    dma("sp", w[:, 0:DT * 128], wchunk(l, seci, ft), [("wfull", l)], [wkg], "k_%s%d" % wkg)
                    dma("sp", w[:, DT * 128:2 * DT * 128], wchunk(l, seci, FT + ft), [("wfull", l)], [wku], "k_%s%d" % wku)
                    bg = pb % 6; bu = (pb + 1) % 6; pb += 2
                    for kt in range(DT):
                        mm(PS(bg)[:, :N], w[:, kt * 128:(kt + 1) * 128], xn[:, kt, :N], kt == 0, kt == DT - 1,
                           [wkg, ("xn", kt)], [pk(bg)])
                    for kt in range(DT):
                        mm(PS(bu)[:, :N], w[:, (DT + kt) * 128:(DT + kt + 1) * 128], xn[:, kt, :N], kt == 0, kt == DT - 1,
                           [wku, ("xn", kt)], [pk(bu)])
                    sg, sgk = sg_rot.next()
                    act(sg[:, :N], PS(bg)[:, :N], AF.Silu, [pk(bg)], [sgk])
                    tt("dve", h1[:, ft, :N], sg[:, :N], PS(bu)[:, :N], ALU.mult, [sgk, pk(bu)], [("h1", ft)])
                for dt in range(DT):
                    w, wk = wo_rot.next()
                    dma("sp", w, wchunk(l, seco, dt), [("wfull", l)], [wk], "k_%s%d" % wk)
                    b = 6 + (dt % 2)
                    for ft in range(FT):
                        mm(PS(b)[:, :N], w[:, ft * 128:(ft + 1) * 128], h1[:, ft, :N], ft == 0, ft == FT - 1,
                           [wk, ("h1", ft)], [pk(b)])
                    stt(hs[:, dt, :N], PS(b)[:, :N], MV(isctx, l, m0 + 2, dt), hs[:, dt, :N], ALU.mult, ALU.add,
                        [pk(b), ("hs", dt)], [("hs", dt)])
                if final_out and not isctx:
                    store(outT[:, :, t0:t0 + N].rearrange("a p n -> p a n"), hs[:, :, :N], hs_keys(), [("outT", t0)], "k_hst")
                else:
                    store(hT[:, :, t0:t0 + N].rearrange("a p n -> p a n"), hs[:, :, :N], hs_keys(), [("hT", t0)], "k_hst")
            S.barrier()

        def mixa_phase(l, par):
            A.off = PERSIST
            xi = XI[par]
            xif = xflat(xi)
            hs = A.f32(DT * 512).rearrange("p (a n) -> p a n", a=DT)
            xn = A.bf(DT * 512).rearrange("p (a n) -> p a n", a=DT)
            sq_rot = Rot([A.f32(512) for _ in range(3)], "sq")
            tmp_rot = Rot([A.f32(512) for _ in range(3)], "tmp")
            rstd = A.f32(512)
            rs2 = Rot([A.f32(512) for _ in range(2)], "rs2")
            win_rot = Rot([A.bf(DT * 128) for _ in range(4)], "win")
            wuq = A.bf(8 * 4 * 128).rearrange("p (c x) -> p c x", c=8)
            wukv = A.bf(8 * 128).rearrange("p (c x) -> p c x", c=8)
            zq = A.f32(4 * 512).rearrange("p (a n) -> p a n", a=4)
            cqn = A.bf(4 * 512).rearrange("p (a n) -> p a n", a=4)
            cgs = A.f32(4 * 512).rearrange("p (a n) -> p a n", a=4)
            rA = A.f32(2 * 512).rearrange("p (a n) -> p a n", a=2)
            rB = A.f32(2 * 512).rearrange("p (a n) -> p a n", a=2)
            raw_rot = Rot([A.f32(512) for _ in range(3)], "raw")
            qn_rot = Rot([A.f32(512) for _ in range(3)], "qn")
            t1_rot = Rot([A.f32(512) for _ in range(2)], "t1")
            t2_rot = Rot([A.f32(512) for _ in range(2)], "t2")
            zkv = A.f32(512); zpe = A.f32(512); sqpe = A.f32(512); base = A.f32(512); ropeb = A.f32(512)
            ckvn = A.bf(512); fzb = A.bf(512)
            st_rot = Rot([A.bf(512) for _ in range(6)], "st")
            stf_rot = Rot([A.f32(512) for _ in range(3)], "stf")
            halst = A.f32(8)
            S.add("dve", lambda e: e.memset(halst, 0.0), [], ["halst"])
            for c in range(8):
                dma("sp", wuq[:, c, :], wchunk(l, "wuq", c), [("wfull", l)], ["wuq"], "k_wuq")
            for c in range(8):
                dma("sp", wukv[:, c, :], wchunk(l, "wukv", c), [("wfull", l)], ["wukv"], "k_wukv")
            GL = l * 4
            pbr = [0]

            def nb():
                b = pbr[0] % 6; pbr[0] += 1
                return b

            def inproj(c, N):
                w, wk = win_rot.next()
                dma("sp", w, wchunk(l, "win", c), [("wfull", l)], [wk], "k_%s%d" % wk)
                flush()
                b = nb()
                for kt in range(DT):
                    mm(PS(b)[:, :N], w[:, kt * 128:(kt + 1) * 128], xn[:, kt, :N], kt == 0, kt == DT - 1,
                       [wk, ("xn", kt)], [pk(b)])
                return b

            def rope(src, srck, Rcol, tab, N, out_bf, outk):
                b = nb()
                mm(PS(b)[:, :N], cst[:, Rcol:Rcol + 128], src[:, :N], True, True, [srck], [pk(b)])
                t1, t1k = t1_rot.next()
                t2, t2k = t2_rot.next()
                tt("dve", t1[:, :N], src[:, :N], tab[:, 0, :N], ALU.mult, [srck, "ropetab"], [t1k])
                tt("dve", t2[:, :N], PS(b)[:, :N], tab[:, 1, :N], ALU.mult, [pk(b), "ropetab"], [t2k])
                tt("dve", out_bf[:, :N], t1[:, :N], t2[:, :N], ALU.add, [t1k, t2k], [outk])

            def store_fm(dst, src_bf, N, srck, rows=128):
                store(dst, src_bf[0:rows, :N], [srck], [("dr", id(dst))], "k_%s%d" % srck)

            for (t0, N, isctx) in GROUPS:
                dma("sp", hs[:, :, :N], hT[:, :, t0:t0 + N].rearrange("a p n -> p a n"), [("hT", t0)], hs_keys(), "k_hs")
                if not isctx:
                    dma("sp", rA[:, :, :N], ropeA[:, :, t0:t0 + N].rearrange("a p n -> p a n"), [], ["ropetab"], "k_rA")
                    dma("sp", rB[:, :, :N], ropeB[:, :, t0:t0 + N].rearrange("a p n -> p a n"), [], ["ropetab"], "k_rA")
                norm_mod(hs, xn, N, l, isctx, 3, 4, sq_rot, rstd, tmp_rot)
                store(xnT[:, :, t0:t0 + N].rearrange("a p n -> p a n"), xn[:, :, :N], xn_keys(), [("xnT", t0)], "k_xnst")
                tl = t0 - NLAT

                for c in range(4):
                    b = inproj(c, N)
                    act(zq[:, c, :N], PS(b)[:, :N], AF.Copy, [pk(b)], [("zq", c)])
                    sq, sqk = sq_rot.next()
                    act(sq[:, :N], PS(b)[:, :N], AF.Square, [pk(b)], [sqk])
                    mm(PS(6)[:, :N], cst[:, C_ONES448:C_ONES448 + 128], sq[:, :N], c == 0, c == 3, [sqk], [pk(6)])
                rstd_from(6, N, rstd, "rstd")
                for c in range(4):
                    stt(cqn[:, c, :N], zq[:, c, :N], cc(C_GCQ + GL + c), rstd[:, :N], ALU.mult, ALU.mult,
                        [("zq", c), "rstd"], [("cqn", c)])
                for h in range(4):
                    raws = []
                    for part in range(2):
                        b = nb()
                        for kt in range(4):
                            mm(PS(b)[:, :N], wuq[:, 2 * h + part, kt * 128:(kt + 1) * 128], cqn[:, kt, :N], kt == 0, kt == 3,
                               ["wuq", ("cqn", kt)], [pk(b)])
                        raw, rk = raw_rot.next()
                        act(raw[:, :N], PS(b)[:, :N], AF.Copy, [pk(b)], [rk])
                        sq, sqk = sq_rot.next()
                        act(sq[:, :N], PS(b)[:, :N], AF.Square, [pk(b)], [sqk])
                        mm(PS(7)[:, :N], cst[:, C_ONES192:C_ONES192 + 128], sq[:, :N], part == 0, part == 1, [sqk], [pk(7)])
                        raws.append((raw, rk))
                    r2, r2k = rs2.next()
                    rstd_from(7, N, r2, r2k)
                    st, stk = st_rot.next()
                    stt(st[:, :N], raws[0][0][:, :N], cc(C_GQAA + l), r2[:, :N], ALU.mult, ALU.mult, [raws[0][1], r2k], [stk])
                    store_fm(QA[h, :, t0:t0 + N], st, N, stk)
                    qn, qnk = qn_rot.next()
                    stt(qn[:, :N], raws[1][0][:, :N], cc(C_GQAB + l), r2[:, :N], ALU.mult, ALU.mult, [raws[1][1], r2k], [qnk])
                    st, stk = st_rot.next()
                    if not isctx:
                        rope(qn, qnk, C_R64, rA, N, st, stk)
                    else:
                        S.add("dve", lambda e, o=st[:, :N], i=qn[:, :N]: e.tensor_copy(out=o, in_=i), [qnk], [stk])
                    store_fm(QB[h, :, t0:t0 + N], st, N, stk, rows=64)

                b = inproj(4, N)
                act(zkv[:, :N], PS(b)[:, :N], AF.Copy, [pk(b)], ["zkv"])
                sq, sqk = sq_rot.next()
                act(sq[:, :N], PS(b)[:, :N], AF.Square, [pk(b)], [sqk])
                mm(PS(6)[:, :N], cst[:, C_ONES128:C_ONES128 + 128], sq[:, :N], True, True, [sqk], [pk(6)])
                rstd_from(6, N, rstd, "rstd")
                stt(ckvn[:, :N], zkv[:, :N], cc(C_GCKV + l), rstd[:, :N], ALU.mult, ALU.mult, ["zkv", "rstd"], ["ckvn"])
                b = inproj(5, N)
                act(zpe[:, :N], PS(b)[:, :N], AF.Copy, [pk(b)], ["zpe"])
                act(sqpe[:, :N], PS(b)[:, :N], AF.Square, [pk(b)], ["sqpe"])
                ts("dve", base[:, :N], zpe[:, :N], cc(C_GKAB + l), ALU.mult, ["zpe"], ["base"])
                if not isctx:
                    rope_f32 = True
                    bb = nb()
                    mm(PS(bb)[:, :N], cst[:, C_R64:C_R64 + 128], base[:, :N], True, True, ["base"], [pk(bb)])
                    t1, t1k = t1_rot.next(); t2, t2k = t2_rot.next()
                    tt("dve", t1[:, :N], base[:, :N], rA[:, 0, :N], ALU.mult, ["base", "ropetab"], [t1k])
                    tt("dve", t2[:, :N], PS(bb)[:, :N], rA[:, 1, :N], ALU.mult, [pk(bb), "ropetab"], [t2k])
                    tt("dve", ropeb[:, :N], t1[:, :N], t2[:, :N], ALU.add, [t1k, t2k], ["ropeb"])
                    kb_src, kb_k = ropeb, "ropeb"
                else:
                    kb_src, kb_k = base, "base"
                for h in range(4):
                    b = nb()
                    mm(PS(b)[:, :N], wukv[:, 2 * h, :], ckvn[:, :N], True, True, ["wukv", "ckvn"], [pk(b)])
                    raw, rk = raw_rot.next()
                    act(raw[:, :N], PS(b)[:, :N], AF.Copy, [pk(b)], [rk])
                    sq, sqk = sq_rot.next()
                    act(sq[:, :N], PS(b)[:, :N], AF.Square, [pk(b)], [sqk])
                    mm(PS(7)[:, :N], cst[:, C_ONES192:C_ONES192 + 128], sq[:, :N], True, False, [sqk], [pk(7)])
                    mm(PS(7)[:, :N], cst[:, C_ONES192:C_ONES192 + 128], sqpe[:, :N], False, True, ["sqpe"], [pk(7)])
                    r2, r2k = rs2.next()
                    rstd_from(7, N, r2, r2k)
                    st, stk = st_rot.next()
                    stt(st[:, :N], raw[:, :N], cc(C_GKAA + l), r2[:, :N], ALU.mult, ALU.mult, [rk, r2k], [stk])
                    if not isctx:
                        dst = xif[X_KA + h * 128 * NLAT:X_KA + (h + 1) * 128 * NLAT].rearrange("(p n) -> p n", p=128)[:, t0:t0 + N]
                    else:
                        dst = KAc[h, :, tl:tl + N]
                    store_fm(dst, st, N, stk)
                    st, stk = st_rot.next()
                    tt("dve", st[:, :N], kb_src[:, :N], r2[:, :N], ALU.mult, [kb_k, r2k], [stk])
                    if not isctx:
                        dst = xif[X_KB + h * 64 * NLAT:X_KB + (h + 1) * 64 * NLAT].rearrange("(p n) -> p n", p=64)[:, t0:t0 + N]
                    else:
                        dst = KBc[h, :, tl:tl + N]
                    store_fm(dst, st, N, stk, rows=64)
                for tsb in range(N // 128):
                    b = nb()
                    for h in range(4):
                        mm(PS(b)[:, h * 128:(h + 1) * 128], ckvn[:, tsb * 128:(tsb + 1) * 128], wukv[:, 2 * h + 1, :], True, True,
                           ["wukv", "ckvn"], [pk(b)])
                    st, stk = st_rot.next()
                    act(st[:, :512], PS(b)[:, :512], AF.Copy, [pk(b)], [stk])
                    if not isctx:
                        dst = xif[X_VA:X_VA + NLAT * 512].rearrange("(t d) -> t d", d=512)[t0 + tsb * 128:t0 + (tsb + 1) * 128, :]
                    else:
                        dst = VAc[tl + tsb * 128:tl + (tsb + 1) * 128, :]
                    store(dst, st[:, :512], [stk], [("dr", "va", t0, tsb)], "k_%s%d" % stk)

                for hh in range(6):
                    b = inproj(6 + hh, N)
                    raw, rk = raw_rot.next()
                    act(raw[:, :N], PS(b)[:, :N], AF.Copy, [pk(b)], [rk])
                    sq, sqk = sq_rot.next()
                    act(sq[:, :N], PS(b)[:, :N], AF.Square, [pk(b)], [sqk])
                    mm(PS(6)[:, :N], cst[:, C_ONES128:C_ONES128 + 128], sq[:, :N], True, True, [sqk], [pk(6)])
                    r2, r2k = rs2.next()
                    rstd_from(6, N, r2, r2k)
                    qn, qnk = qn_rot.next()
                    gcol = (C_GQB if hh < 4 else C_GKB) + l
                    stt(qn[:, :N], raw[:, :N], cc(gcol), r2[:, :N], ALU.mult, ALU.mult, [rk, r2k], [qnk])
                    st, stk = st_rot.next()
                    if not isctx:
                        rope(qn, qnk, C_R128, rB, N, st, stk)
                    else:
                        S.add("dve", lambda e, o=st[:, :N], i=qn[:, :N]: e.tensor_copy(out=o, in_=i), [qnk], [stk])
                    if hh < 4:
                        dst = QG[hh, :, t0:t0 + N]
                    elif not isctx:
                        kh = hh - 4
                        dst = xif[X_KG + kh * 128 * NLAT:X_KG + (kh + 1) * 128 * NLAT].rearrange("(p n) -> p n", p=128)[:, t0:t0 + N]
                    else:
                        dst = KGc[hh - 4, :, tl:tl + N]
                    store_fm(dst, st, N, stk)
                for cv in range(2):
                    w, wk = win_rot.next()
                    dma("sp", w, wchunk(l, "win", 12 + cv), [("wfull", l)], [wk], "k_%s%d" % wk)
                    b = nb()
                    nts = N // 128
                    for tsb in range(nts):
                        for kt in range(DT):
                            mm(PS(b)[:, tsb * 128:(tsb + 1) * 128], xn[:, kt, tsb * 128:(tsb + 1) * 128], w[:, kt * 128:(kt + 1) * 128],
                               kt == 0, kt == DT - 1, [wk, ("xn", kt)], [pk(b)])
                    st, stk = st_rot.next()
                    act(st[:, :N], PS(b)[:, :N], AF.Copy, [pk(b)], [stk])
                    if not isctx:
                        dstv = xif[X_VG:X_VG + NLAT * 256].rearrange("(t d) -> t d", d=256)[t0:t0 + N, cv * 128:(cv + 1) * 128]
                    else:
                        dstv = VGc[tl:tl + N, cv * 128:(cv + 1) * 128]
                    store(dstv.rearrange("(a p) d -> p a d", p=128), st[:, :N].rearrange("p (a d) -> p a d", d=128),
                        [stk], [("dr", "vg", t0, cv)], "k_%s%d" % stk)

                for ccx in range(4):
                    b = inproj(14 + ccx, N)
                    sf, sfk = stf_rot.next()
                    act(sf[:, :N], PS(b)[:, :N], AF.Copy, [pk(b)], [sfk])
                    store(CB[ccx, :, t0:t0 + N], sf[:, :N], [sfk], [("dr", "cb", t0, ccx)], "k_%s%d" % sfk)
                for ccx in range(4):
                    b = inproj(18 + ccx, N)
                    act(cgs[:, ccx, :N], PS(b)[:, :N], AF.Copy, [pk(b)], [("cgs", ccx)])
                for ccx in range(4):
                    b = inproj(22 + ccx, N)
                    sf, sfk = stf_rot.next()
                    tt("dve", sf[:, :N], cgs[:, ccx, :N], PS(b)[:, :N], ALU.mult, [("cgs", ccx), pk(b)], [sfk])
                    if not isctx:
                        store(UU[ccx, :, 1 + t0:1 + t0 + N], sf[:, :N], [sfk], [("dr", "uu", t0, ccx)], "k_%s%d" % sfk)
                        if t0 == 0:
                            S.add("dve", lambda e, o=halst[:, 2 * ccx:2 * ccx + 1], i=sf[:, 0:1]: e.tensor_copy(out=o, in_=i), [sfk, "halst"], ["halst"])
                        if t0 + N == NLAT:
                            S.add("dve", lambda e, o=halst[:, 2 * ccx + 1:2 * ccx + 2], i=sf[:, N - 1:N]: e.tensor_copy(out=o, in_=i), [sfk, "halst"], ["halst"])
                    else:
                        store(UUc[ccx, :, 1 + tl:1 + tl + N], sf[:, :N], [sfk], [("dr", "uu", t0, ccx)], "k_%s%d" % sfk)

                for G in range(4):
                    b = inproj(26 + G, N)
                    act(fzb[:, :N], PS(b)[:, :N], AF.Copy, [pk(b)], ["fzb"])
                    for half in range(max(1, N // 256)):
                        b2 = nb()
                        for t2i in range(2):
                            tsb = half * 2 + t2i
                            mm(PS(b2)[:, t2i * 256:(t2i + 1) * 256], fzb[:, tsb * 128:(tsb + 1) * 128], cdft_b, True, True,
                               ["fzb"], [pk(b2)])
                        st, stk = st_rot.next()
                        act(st[:, :512], PS(b2)[:, :512], AF.Copy, [pk(b2)], [stk])
                        if not isctx:
                            dsta = xif[X_AB:X_AB + NLAT * 1024].rearrange("(t d) -> t d", d=1024)[t0 + half * 256:t0 + (half + 1) * 256, G * 256:(G + 1) * 256]
                        else:
                            dsta = ABc[tl + half * 256:tl + (half + 1) * 256, G * 256:(G + 1) * 256]
                        store(dsta.rearrange("(a p) d -> p a d", p=128), st[:, :512].rearrange("p (a d) -> p a d", d=256),
                            [stk], [("dr", "ab", t0, G, half)], "k_%s%d" % stk)
            store(xif[XSZ:XSZ2].bitcast(F32).rearrange("(p x) -> p x", p=128), halst, ["halst"], [("HI", par)], "k_hist")
            S.barrier()
            for u in range(NU):
                S.add("pool", lambda e, u=u: e.collective_compute("AllGather", ALU.bypass, replica_groups=PAIRS,
                                                                  ins=[XI[par][u * 128:(u + 1) * 128, :].opt()],
                                                                  outs=[XO[par][u * 256:(u + 1) * 256, :].opt()]),
                      [], [("XO", par)], dma_key="k_xag", inc=1)
            if l + 1 < depth:
                weights_stage(l + 1)
            S.barrier()

        def attn_phase(l, par, do_ctx):
            A.off = PERSIST
            xof = xflat(XO[par])
            KA = Rot([A.bf(4352) for _ in range(2)], "KAs")
            KB = Rot([A.bf(4352) for _ in range(2)], "KBs")
            V = A.bf(34 * 512).rearrange("p (c d) -> p c d", d=512)
            Qa = Rot([A.bf(512) for _ in range(2)], "Qa")
            Qb = Rot([A.bf(512) for _ in range(2)], "Qb")
            PT = Rot([A.bf(512) for _ in range(4)], "PT")
            OS = Rot([A.bf(512) for _ in range(2)], "OS")
            rinv = Rot([A.f32(512) for _ in range(2)], "rinv")
            sb_i = [0]; ob_i = [0]

            def xo_ap(r, off, n):
                u = off // UE; w = off % UE
                assert w + n <= UE
                b0 = u * 2 * UE + r * UE + w
                return xof[b0:b0 + n]

            def load_k(dst, sec, h, rows, dk, ctxsrc):
                per = rows * NLAT
                for r in range(2):
                    for hh in range(per // UE):
                        src = xo_ap(r, sec + h * per + hh * UE, UE).rearrange("(p n) -> p n", n=NLAT)
                        dma("sp", dst[hh * 64:(hh + 1) * 64, r * NLAT:(r + 1) * NLAT], src, [("XO", par)], [dk],
                            "k_%s%d_%d" % (dk[0], dk[1], (r * 2 + hh) % 2))
                dma("sp", dst[0:rows, 2 * NLAT:2 * NLAT + NCTX], ctxsrc[h], [], [dk], "k_%s%d_c" % dk)

            def load_v(sec, width, ctxsrc):
                Vv = V[:, :, 0:width]
                cu = UE // width // 128
                for r in range(2):
                    for j in range(NLAT * width // UE):
                        src = xo_ap(r, sec + j * UE, UE).rearrange("(c p d) -> p c d", p=128, d=width)
                        dma("sp", Vv[:, r * 16 + j * cu:r * 16 + (j + 1) * cu, :], src, [("XO", par)], ["V"], "k_V%d" % (j % 2))
                dma("sp", Vv[:, 32:34, :], ctxsrc.rearrange("(c p) d -> p c d", p=128), [], ["V"], "k_Vc")

            def heads(branch, nh, scale, mla):
                for h in range(nh):
                    if mla or h % 2 == 0:
                        ka, kak = KA.next()
                        if mla:
                            load_k(ka, X_KA, h, 128, kak, KAc)
                            kb, kbk = KB.next()
                            load_k(kb, X_KB, h, 64, kbk, KBc)
                        else:
                            load_k(ka, X_KG, h // 2, 128, kak, KGc)
                    vh = h if mla else h // 2
                    for (t0, N, isctx) in GROUPS:
                        if isctx and not do_ctx:
                            continue
                        qa, qak = Qa.next()
                        dma("sp", qa[:, :N], (QA if mla else QG)[h, :, t0:t0 + N], [], [qak], "k_%s%d" % qak)
                        flush()
                        if mla:
                            qb, qbk = Qb.next()
                            dma("sp", qb[0:64, :N], QB[h, :, t0:t0 + N], [], [qbk], "k_%s%d" % qbk)
                        chunks = list(range(32, 34)) if isctx else list(range(34))
                        ob = 4 + (ob_i[0] % 2); lb = 6 + (ob_i[0] % 2); ob_i[0] += 1
                        def emit_S(kc):
                            sb = sb_i[0] % 4; sb_i[0] += 1
                            mm(PS(sb)[:, :N], ka[:, kc * 128:(kc + 1) * 128], qa[:, :N], True, not mla, [kak, qak], [pk(sb)])
                            if mla:
                                mm(PS(sb)[:, :N], kb[0:64, kc * 128:(kc + 1) * 128], qb[0:64, :N], False, True, [kbk, qbk], [pk(sb)])
                            return sb
                        LA = 2
                        sbs = [emit_S(kc) for kc in chunks[:LA]]
                        for ci, kc in enumerate(chunks):
                            if ci + LA < len(chunks):
                                sbs.append(emit_S(chunks[ci + LA]))
                            sb = sbs[ci]
                            pt, ptk = PT.next()
                            act(pt[:, :N], PS(sb)[:, :N], AF.Exp, [pk(sb)], [ptk], scale=scale)
                            first = ci == 0; last = ci == len(chunks) - 1
                            mm(PS(ob)[:, :N], V[:, kc, vh * 128:(vh + 1) * 128], pt[:, :N], first, last, ["V", ptk], [pk(ob)])
                            mm(PS(lb)[:, :N], ones_b, pt[:, :N], first, last, [ptk], [pk(lb)])
                        ri, rik = rinv.next()
                        S.add("dve", lambda e, o=ri[:, :N], i=PS(lb)[:, :N]: e.reciprocal(out=o, in_=i), [pk(lb)], [rik])
                        os_, osk = OS.next()
                        tt("dve", os_[:, :N], PS(ob)[:, :N], ri[:, :N], ALU.mult, [pk(ob), rik], [osk])
                        store(BR[branch * 4 + h, :, t0:t0 + N], os_[:, :N], [osk], [("dr", "br", branch, h, t0)], "k_%s%d" % osk)

            load_v(X_VA, 512, VAc)
            heads(0, 4, 192 ** -0.5, True)
            load_v(X_VG, 256, VGc)
            heads(1, 4, 128 ** -0.5, False)
            S.barrier()

        def cf_phase(l, par, do_ctx):
            A.off = PERSIST
            xof = xflat(XO[par])
            AB = A.bf(34 * 1024).rearrange("p (c d) -> p c d", d=1024)
            tabc = Rot([A.bf(8 * 512) for _ in range(2)], "tabc")
            tabs = Rot([A.bf(8 * 512) for _ in range(2)], "tabs")
            cb = A.f32(4 * 512).rearrange("p (a n) -> p a n", a=4)
            U3 = A.f32(4 * 514).rearrange("p (a n) -> p a n", a=4)
            ta = Rot([A.f32(512) for _ in range(2)], "ta")
            tb = Rot([A.f32(512) for _ in range(2)], "tb")
            st_rot = Rot([A.bf(512) for _ in range(4)], "st")
            hl = A.f32(16)
            def xo_ap(r, off, n):
                u = off // UE; w = off % UE
                assert w + n <= UE
                b0 = u * 2 * UE + r * UE + w
                return xof[b0:b0 + n]
            for r in range(2):
                for j in range(16):
                    src = xo_ap(r, X_AB + j * UE, UE).rearrange("(c p d) -> p c d", p=128, d=1024)
                    dma("sp", AB[:, r * 16 + j:r * 16 + j + 1, :], src, [("XO", par)], ["AB"], "k_AB%d" % (j % 2))
            dma("sp", AB[:, 32:34, :], ABc.rearrange("(c p) d -> p c d", p=128), [], ["AB"], "k_ABc")
            for r in range(2):
                dma("sp", hl[:, r * 8:(r + 1) * 8], xo_ap(r, XSZ, 2048).bitcast(F32).rearrange("(p x) -> p x", p=128),
                    [("XO", par)], ["hl"], "k_hl")
            hv = hal.rearrange("p (c x) -> p c x", x=2)
            hlv = hl.rearrange("p (r c x) -> p r c x", r=2, x=2)
            ts("dve", hv[:, :, 0], hlv[:, 0, :, 1], cc(C_HMASK), ALU.mult, ["hl"], ["hal"])
            ts("dve", hv[:, :, 1], hlv[:, 1, :, 0], cc(C_HMASK + 1), ALU.mult, ["hl", "hal"], ["hal"])
            for (t0, N, isctx) in GROUPS:
                if isctx and not do_ctx:
                    continue
                tl = t0 - NLAT
                dma("sp", cb[:, :, :N], CB[:, :, t0:t0 + N].rearrange("a p n -> p a n"), [], ["cb"], "k_cb")
                if not isctx:
                    dma("sp", U3[:, :, :N + 2], UU[:, :, t0:t0 + N + 2].rearrange("a p n -> p a n"), [], ["U3"], "k_u3")
                    if t0 == 0:
                        S.add("dve", lambda e: e.tensor_copy(out=U3[:, :, 0], in_=hv[:, :, 0]), ["hal", "U3"], ["U3"])
                    if t0 + N == NLAT:
                        S.add("dve", lambda e, N=N: e.tensor_copy(out=U3[:, :, N + 1], in_=hv[:, :, 1]), ["hal", "U3"], ["U3"])
                else:
                    dma("sp", U3[:, :, :N + 2], UUc[:, :, 0:N + 2].rearrange("a p n -> p a n"), [], ["U3"], "k_u3")
                    S.add("dve", lambda e: e.memset(U3[:, :, 0], 0.0), ["U3"], ["U3"])
                    S.add("dve", lambda e, N=N: e.memset(U3[:, :, N + 1], 0.0), ["U3"], ["U3"])
                for ccx in range(4):
                    wc = C_CONVW + l * 12
                    a1, a1k = ta.next(); b1, b1k = tb.next()
                    ts("dve", a1[:, :N], U3[:, ccx, 0:N], cc(wc + 0 * 4 + ccx), ALU.mult, ["U3"], [a1k])
                    stt(b1[:, :N], U3[:, ccx, 1:N + 1], cc(wc + 1 * 4 + ccx), a1[:, :N], ALU.mult, ALU.add, ["U3", a1k], [b1k])
                    a2, a2k = ta.next()
                    stt(a2[:, :N], U3[:, ccx, 2:N + 2], cc(wc + 2 * 4 + ccx), b1[:, :N], ALU.mult, ALU.add, ["U3", b1k], [a2k])
                    st, stk = st_rot.next()
                    stt(st[:, :N], a2[:, :N], cc(C_CONVB + l * 4 + ccx), cb[:, ccx, :N], ALU.add, ALU.mult, [a2k, "cb"], [stk])
                    store(BR[8 + ccx, :, t0:t0 + N], st[:, :N], [stk], [("dr", "br2", ccx, t0)], "k_%s%d" % stk)
                if not isctx:
                    g = t0 // 512
                    for blk in range(4):
                        tc_, tck = tabc.next(); tsn, tsk = tabs.next()
                        dma("sp", tc_.rearrange("p (c k) -> p c k", c=8), fcos[g, :, blk * 8:(blk + 1) * 8, :], [], [tck], "k_%s%d" % tck)
                        dma("sp", tsn.rearrange("p (c k) -> p c k", c=8), fsin[g, :, blk * 8:(blk + 1) * 8, :], [], [tsk], "k_%s%d" % tsk)
                        for ti in range(8):
                            tcn = blk * 8 + ti
                            for G in range(4):
                                mm(PS(G)[:, :N], AB[:, tcn, G * 256:G * 256 + 128], tc_[:, ti * 512:(ti + 1) * 512], tcn == 0, False,
                                   ["AB", tck], [pk(G)])
                                mm(PS(G)[:, :N], AB[:, tcn, G * 256 + 128:G * 256 + 256], tsn[:, ti * 512:(ti + 1) * 512], False, tcn == 31,
                                   ["AB", tsk], [pk(G)])
                else:
                    tc_, tck = tabc.next(); tsn, tsk = tabs.next()
                    dma("sp", tc_[:, 0:512].rearrange("p (c k) -> p c k", c=2), fcosc, [], [tck], "k_%s%d" % tck)
                    dma("sp", tsn[:, 0:512].rearrange("p (c k) -> p c k", c=2), fsinc, [], [tsk], "k_%s%d" % tsk)
                    for ti in range(2):
                        for G in range(4):
                            mm(PS(G)[:, :N], AB[:, 32 + ti, G * 256:G * 256 + 128], tc_[:, ti * 256:(ti + 1) * 256], ti == 0, False,
                               ["AB", tck], [pk(G)])
                            mm(PS(G)[:, :N], AB[:, 32 + ti, G * 256 + 128:G * 256 + 256], tsn[:, ti * 256:(ti + 1) * 256], False, ti == 1,
                               ["AB", tsk], [pk(G)])
                for G in range(4):
                    st, stk = st_rot.next()
                    act(st[:, :N], PS(G)[:, :N], AF.Copy, [pk(G)], [stk])
                    store(BR[12 + G, :, t0:t0 + N], st[:, :N], [stk], [("dr", "br3", G, t0)], "k_%s%d" % stk)
            S.barrier()

        def merge_phase(l, do_ctx):
            A.off = PERSIST
            hs = A.f32(DT * 512).rearrange("p (a n) -> p a n", a=DT)
            xn = A.bf(DT * 512).rearrange("p (a n) -> p a n", a=DT)
            br = A.bf(16 * 512).rearrange("p (a n) -> p a n", a=16)
            mg = A.bf(DT * 512).rearrange("p (a n) -> p a n", a=DT)
            wg_rot = Rot([A.bf(DT * 128) for _ in range(4)], "wg")
            wb_rot = Rot([A.bf(4 * 128) for _ in range(4)], "wb")
            wo_rot = Rot([A.bf(DT * 128) for _ in range(2)], "wo")
            sg_rot = Rot([A.f32(512) for _ in range(3)], "sg")
            acc_rot = Rot([A.f32(512) for _ in range(3)], "acc")
            tmp_rot = Rot([A.f32(512) for _ in range(3)], "tmp")
            pb = [0]
            for (t0, N, isctx) in GROUPS:
                if isctx and not do_ctx:
                    continue
                dma("sp", hs[:, :, :N], hT[:, :, t0:t0 + N].rearrange("a p n -> p a n"), [("hT", t0)], hs_keys(), "k_hs")
                dma("sp", xn[:, :, :N], xnT[:, :, t0:t0 + N].rearrange("a p n -> p a n"), [], xn_keys(), "k_xn")
                dma("sp", br[:, :, :N], BR[:, :, t0:t0 + N].rearrange("a p n -> p a n"), [], ["br"], "k_br")
                for dt in range(DT):
                    prev = None
                    for i in range(4):
                        w, wk = wg_rot.next()
                        dma("sp", w, wchunk(l, "wg", i * 16 + dt), [("wfull", l)], [wk], "k_%s%d" % wk)
                        w2, w2k = wb_rot.next()
                        dma("sp", w2, wchunk(l, "wbr", i * 16 + dt), [("wfull", l)], [w2k], "k_%s%d" % w2k)
                        bg = pb[0] % 6; bb = (pb[0] + 1) % 6; pb[0] += 2
                        for kt in range(DT):
                            mm(PS(bg)[:, :N], w[:, kt * 128:(kt + 1) * 128], xn[:, kt, :N], kt == 0, kt == DT - 1,
                               [wk, ("xn", kt)], [pk(bg)])
                        for c in range(4):
                            mm(PS(bb)[:, :N], w2[:, c * 128:(c + 1) * 128], br[:, i * 4 + c, :N], c == 0, c == 3,
                               [w2k, "br"], [pk(bb)])
                        sg, sgk = sg_rot.next()
                        act(sg[:, :N], PS(bg)[:, :N], AF.Sigmoid, [pk(bg)], [sgk], bias=cc(C_BGATE + l * 64 + i * 16 + dt))
                        if i == 0:
                            ac, ack = acc_rot.next()
                            tt("dve", ac[:, :N], sg[:, :N], PS(bb)[:, :N], ALU.mult, [sgk, pk(bb)], [ack])
                            prev = (ac, ack)
                        else:
                            tm, tmk = tmp_rot.next()
                            tt("dve", tm[:, :N], sg[:, :N], PS(bb)[:, :N], ALU.mult, [sgk, pk(bb)], [tmk])
                            if i < 3:
                                ac, ack = acc_rot.next()
                                tt("dve", ac[:, :N], prev[0][:, :N], tm[:, :N], ALU.add, [prev[1], tmk], [ack])
                                prev = (ac, ack)
                            else:
                                tt("dve", mg[:, dt, :N], prev[0][:, :N], tm[:, :N], ALU.add, [prev[1], tmk], [("mg", dt)])
                for dt in range(DT):
                    w, wk = wo_rot.next()
                    dma("sp", w, wchunk(l, "wo", dt), [("wfull", l)], [wk], "k_%s%d" % wk)
                    b = 6 + dt % 2
                    for kt in range(DT):
                        mm(PS(b)[:, :N], w[:, kt * 128:(kt + 1) * 128], mg[:, kt, :N], kt == 0, kt == DT - 1,
                           [wk, ("mg", kt)], [pk(b)])
                    stt(hs[:, dt, :N], PS(b)[:, :N], MV(isctx, l, 5, dt), hs[:, dt, :N], ALU.mult, ALU.add,
                        [pk(b), ("hs", dt)], [("hs", dt)])
                store(hT[:, :, t0:t0 + N].rearrange("a p n -> p a n"), hs[:, :, :N], hs_keys(), [("hT", t0)], "k_hst")
            S.barrier()

        phases = []
        for l in range(depth):
            last = l == depth - 1
            grp_all = GROUPS
            grp_lat = GROUPS[:4]
            phases.append(("ffn1", l, lambda l=l: ffn_phase(l, 1, grp_all, False)))
            phases.append(("mixa", l, lambda l=l: mixa_phase(l, l % 2)))
            phases.append(("attn", l, lambda l=l, last=last: attn_phase(l, l % 2, not last)))
            phases.append(("cf", l, lambda l=l, last=last: cf_phase(l, l % 2, not last)))
            phases.append(("merge", l, lambda l=l, last=last: merge_phase(l, not last)))
            phases.append(("ffn2", l, lambda l=l, last=last: ffn_phase(l, 2, grp_lat if last else grp_all, last)))
        for name, l, fn in phases:
            if stop_after is not None and stop_after[0] == "prologue":
                break
            fn()
            if stop_after is not None and (name, l) == tuple(stop_after):
                break
        if stop_after is not None:
            A.off = PERSIST
            S.barrier()
            store(outT, hT[:, :, 0:NLAT], [], ["outT_final"], "k_hst")
        S.barrier()

        S.check()
        keys = sorted(S.dma_cnt.keys())
        sems_k = {}
        for k in keys:
            sems_k[k] = es.enter_context(nc.semaphore("s_" + k))
        sems_e = {e: es.enter_context(nc.semaphore("e_" + e)) for e in Sched.ENGS}
        block = es.enter_context(nc.Block())
        S.emit(nc, block, None, sems_e, sems_k)
    return nc


def _tile_w(W, n_chunks_pad):
    K, N = W.shape
    KT = (K + 127) // 128
    NT = (N + 127) // 128
    Wp = np.zeros((KT * 128, n_chunks_pad * 128), np.float32)
    Wp[:K, :N] = W
    return np.ascontiguousarray(Wp.reshape(KT, 128, n_chunks_pad, 128).transpose(2, 1, 0, 3))


def _layer_sections(inp, l):
    secs = {}
    secs["wi1"] = _tile_w(inp["ffn1_wi"][l], 88)
    secs["wo1"] = _tile_w(inp["ffn1_wo"][l], 16)
    secs["wi2"] = _tile_w(inp["ffn2_wi"][l], 88)
    secs["wo2"] = _tile_w(inp["ffn2_wo"][l], 16)
    w_in = inp["w_in"][l]
    cols = np.zeros((D, 32 * 128), np.float32)
    cols[:, 0:448] = w_in[:, 0:448]
    cols[:, 512:640] = w_in[:, 448:576]
    cols[:, 640:704] = w_in[:, 576:640]
    cols[:, 768:768 + 3072] = w_in[:, 640:3712]
    secs["win"] = _tile_w(cols, 32)
    wg = inp["w_gate"][l]
    secs["wg"] = np.concatenate([_tile_w(wg[i], 16) for i in range(4)], 0)
    wb = inp["w_branch"][l]
    secs["wbr"] = np.concatenate([_tile_w(wb[i], 16) for i in range(4)], 0)
    secs["wo"] = _tile_w(inp["w_o"][l], 16)
    wuq = inp["w_uq"][l]
    c2 = np.zeros((448, 8 * 128), np.float32)
    for h in range(4):
        c2[:, (2 * h) * 128:(2 * h) * 128 + 128] = wuq[:, h * 192:h * 192 + 128]
        c2[:, (2 * h + 1) * 128:(2 * h + 1) * 128 + 64] = wuq[:, h * 192 + 128:h * 192 + 192]
    secs["wuq"] = _tile_w(c2, 8)
    secs["wukv"] = _tile_w(inp["w_ukv"][l], 8)
    return secs


def _prep_inputs(inp, depth):
    f32 = np.float32
    per_core = [dict() for _ in range(8)]
    wsh = np.zeros((8, depth, SHARD), f32)
    for l in range(depth):
        secs = _layer_sections(inp, l)
        for name, ncp, kt in SECS:
            off, per, celems, _ = SEC_INFO[name]
            arr = secs[name].reshape(ncp, celems)
            for r in range(8):
                wsh[r, l, off:off + per * celems] = arr[r * per:(r + 1) * per].reshape(-1)
        del secs
    wada_all = inp["w_ada"][:depth]
    base = np.zeros((128, NCONST), f32)
    base[:, C_ONESD:C_ONESD + 128] = 1.0 / 2048
    base[:, C_ONES448:C_ONES448 + 128] = 1.0 / 448
    base[:, C_ONES192:C_ONES192 + 128] = 1.0 / 192
    base[:, C_ONES128:C_ONES128 + 128] = 1.0 / 128
    base[:, C_ONE1:C_ONE1 + 128] = 1.0
    base[:, C_EPS] = EPS

    def rmat(half):
        Rm = np.zeros((128, 128), f32)
        for r in range(half):
            Rm[r, r + half] = -1.0
            Rm[r + half, r] = 1.0
        return Rm.T.copy()
    base[:, C_R64:C_R64 + 128] = rmat(32)
    base[:, C_R128:C_R128 + 128] = rmat(64)
    cidx = np.arange(128)
    ang = 2 * np.pi * np.outer(cidx, cidx) / 128.0
    base[:, C_CDFT:C_CDFT + 128] = np.cos(ang)
    base[:, C_CDFT + 128:C_CDFT + 256] = np.sin(ang)

    def fm(v):
        return v[:depth].reshape(depth, 16, 128).transpose(2, 0, 1).reshape(128, depth * 16)
    base[:, C_NORM1:C_NORM1 + depth * 16] = fm(inp["norm_ffn1"])
    base[:, C_NORMM:C_NORMM + depth * 16] = fm(inp["norm_mix"])
    base[:, C_NORM2:C_NORM2 + depth * 16] = fm(inp["norm_ffn2"])
    gcq = np.zeros((depth, 512), f32); gcq[:, :448] = inp["g_cq"][:depth]
    base[:, C_GCQ:C_GCQ + depth * 4] = gcq.reshape(depth, 4, 128).transpose(2, 0, 1).reshape(128, depth * 4)
    base[:, C_GCKV:C_GCKV + depth] = inp["g_ckv"][:depth].T
    base[:, C_GQAA:C_GQAA + depth] = inp["g_qa"][:depth, :128].T
    base[:64, C_GQAB:C_GQAB + depth] = inp["g_qa"][:depth, 128:].T
    base[:, C_GKAA:C_GKAA + depth] = inp["g_ka"][:depth, :128].T
    base[:64, C_GKAB:C_GKAB + depth] = inp["g_ka"][:depth, 128:].T
    base[:, C_GQB:C_GQB + depth] = inp["g_qb"][:depth].T
    base[:, C_GKB:C_GKB + depth] = inp["g_kb"][:depth].T
    cw = inp["conv_w"][:depth]
    base[:, C_CONVW:C_CONVW + depth * 12] = cw.reshape(depth, 3, 4, 128).transpose(3, 0, 1, 2).reshape(128, depth * 12)
    base[:, C_CONVB:C_CONVB + depth * 4] = inp["conv_b"][:depth].reshape(depth, 4, 128).transpose(2, 0, 1).reshape(128, depth * 4)
    bg = inp["b_gate"][:depth]
    base[:, C_BGATE:C_BGATE + depth * 64] = bg.reshape(depth, 4, 16, 128).transpose(3, 0, 1, 2).reshape(128, depth * 64)
    cc5 = np.concatenate([inp["c"], inp["c_ctx"][None, :]], 0)
    ccT = np.ascontiguousarray(cc5.T.reshape(16, 128, 5).transpose(1, 0, 2))
    pos = np.arange(SEQ)
    prow = (pos // 64).astype(f32); pcol = (pos % 64).astype(f32)

    def rope_tab(rot_dim):
        n = rot_dim // 4
        inv = (10000.0 ** (-np.arange(n, dtype=f32) / n)).astype(f32)
        ang = np.concatenate([prow[:, None] * inv, pcol[:, None] * inv], -1).astype(f32)
        c = np.cos(ang).astype(f32); s = np.sin(ang).astype(f32)
        half = rot_dim // 2
        tab = np.zeros((2, 128, SEQ), f32)
        tab[0, :half] = c.T; tab[0, half:rot_dim] = c.T
        tab[1, :half] = s.T; tab[1, half:rot_dim] = s.T
        return tab
    tabA = rope_tab(64); tabB = rope_tab(128)
    bf = ml_dtypes.bfloat16
    sc_l = 1.0 / np.sqrt(SEQ * 128.0)
    tt_ = np.arange(SEQ, dtype=np.int64)
    sc_c = 1.0 / np.sqrt(NCTX * 128.0)
    tcx = np.arange(NCTX, dtype=np.int64)
    angc = 2 * np.pi * ((np.outer(tcx, tcx)) % NCTX) / NCTX
    fcosc = (np.cos(angc) * sc_c).astype(f32).reshape(2, 128, 256).transpose(1, 0, 2).astype(bf)
    fsinc = (-np.sin(angc) * sc_c).astype(f32).reshape(2, 128, 256).transpose(1, 0, 2).astype(bf)
    for core in range(8):
        b = core // 2; s = core % 2
        d = per_core[core]
        d["wsh"] = wsh[core].reshape(depth, 128, SHW)
        xs = inp["x"][b, s * NLAT:(s + 1) * NLAT, :]
        d["xT"] = np.ascontiguousarray(xs.T.reshape(16, 128, NLAT))
        d["cxT"] = np.ascontiguousarray(inp["ctx"][b].T.reshape(16, 128, NCTX))
        d["ccT"] = ccT
        cst = base.copy()
        wa = wada_all[:, :, core * 2304:(core + 1) * 2304]
        d["wada"] = np.ascontiguousarray(wa.reshape(depth, 16, 128, 18, 128).transpose(0, 3, 2, 1, 4)).reshape(depth * 18, 128, 16 * 128)
        ba = inp["b_ada"][:depth, core * 2304:(core + 1) * 2304]
        cst[:, C_BADA:C_BADA + depth * 18] = ba.reshape(depth, 18, 128).transpose(2, 0, 1).reshape(128, depth * 18)
        cst[:, C_OH + b] = 1.0
        cst[:, C_HMASK + 0] = 1.0 if s == 1 else 0.0
        cst[:, C_HMASK + 1] = 1.0 if s == 0 else 0.0
        d["consts"] = cst
        d["ropeA"] = np.ascontiguousarray(tabA[:, :, s * NLAT:(s + 1) * NLAT])
        d["ropeB"] = np.ascontiguousarray(tabB[:, :, s * NLAT:(s + 1) * NLAT])
        kk = np.arange(s * NLAT, (s + 1) * NLAT, dtype=np.int64)
        ang = 2 * np.pi * ((np.outer(tt_, kk)) % SEQ) / SEQ
        co = (np.cos(ang) * sc_l).astype(f32); si = (-np.sin(ang) * sc_l).astype(f32)
        d["fcos"] = np.ascontiguousarray(co.reshape(32, 128, 4, 512).transpose(2, 1, 0, 3)).astype(bf)
        d["fsin"] = np.ascontiguousarray(si.reshape(32, 128, 4, 512).transpose(2, 1, 0, 3)).astype(bf)
        d["fcosc"] = np.ascontiguousarray(fcosc)
        d["fsinc"] = np.ascontiguousarray(fsinc)
    return per_core


_CACHE = {}


def run(inputs, depth=4, stop_after=None, debug=False):
    inp = {k: np.asarray(v) for k, v in inputs.items()}
    key = (depth, tuple(stop_after) if stop_after else None, debug)
    if key not in _CACHE:
        _CACHE[key] = build_program(depth, stop_after, debug)
    nc = _CACHE[key]
    in_maps = _prep_inputs(inp, depth)
    res = run_bass_kernel_spmd(nc, in_maps, core_ids=list(range(8)))
    return res


def kernel(**inputs):
    res = run(inputs, 4)
    out = np.zeros((4, SEQ, D), np.float32)
    for core in range(8):
        b = core // 2; s = core % 2
        oT = np.asarray(res.results[core]["outT"]).reshape(D, NLAT)
        out[b, s * NLAT:(s + 1) * NLAT, :] = oT.T
    return out
```
